# Optimizing a Trainium2 kernel written in Bass

```python
import math
import jax, jax.numpy as jnp
from jax import lax
import numpy as np


D_MODEL = 1024
BATCH = 8
SEQ = 2048
DEPTH = 2

SSM_WIDTH = D_MODEL // 2
SSM_GROUP = 16
SSM_GROUPS = SSM_WIDTH // SSM_GROUP
SSM_STATE = 64
DT_MIN = 1e-3
DT_MAX = 1e-1
CONV_WIDTH = D_MODEL // 2
CONV_KERNEL = 31
POOL_WIDTH = D_MODEL // 2
POOL_WINDOWS = (2, 4, 8, 16)
POOL_GROUP = POOL_WIDTH // len(POOL_WINDOWS)
N_BRANCHES = 3
IN_WIDTH = SSM_WIDTH + 2 * CONV_WIDTH + POOL_WIDTH + N_BRANCHES * D_MODEL
FFN_HIDDEN = ((8 * D_MODEL + 3 * 256 - 1) // (3 * 256)) * 256
EPS = 1e-6

kernel_name = 'hybrid_s5_conformer_pool_gated_block'


def rms_norm(x, g):
    xf = x.astype(jnp.float32)
    y = xf * lax.rsqrt(jnp.mean(xf * xf, axis=-1, keepdims=True) + EPS)
    return (y * g.astype(jnp.float32)).astype(x.dtype)


def layer_norm(x, g, b):
    xf = x.astype(jnp.float32)
    mu = jnp.mean(xf, axis=-1, keepdims=True)
    var = jnp.mean(jnp.square(xf - mu), axis=-1, keepdims=True)
    y = (xf - mu) * lax.rsqrt(var + EPS)
    return (y * g.astype(jnp.float32) + b.astype(jnp.float32)).astype(x.dtype)


def _complex_linear_combine(e1, e2):
    a1r, a1i, b1r, b1i = e1
    a2r, a2i, b2r, b2i = e2
    return (a2r * a1r - a2i * a1i,
            a2r * a1i + a2i * a1r,
            a2r * b1r - a2i * b1i + b2r,
            a2r * b1i + a2i * b1r + b2i)


def s5_mixer(u, a_re, a_im, log_dt, b_re, b_im, c_re, c_im, d_skip, w_glu, b_glu):
    bsz, seq, _ = u.shape
    f32 = jnp.float32
    uf = u.astype(f32).reshape(bsz, seq, SSM_GROUPS, SSM_GROUP)
    a_re = a_re.astype(f32)
    a_im = a_im.astype(f32)
    dt = jnp.exp(log_dt.astype(f32))[:, None]
    mag = jnp.exp(dt * a_re)
    ang = dt * a_im
    abar_re = mag * jnp.cos(ang)
    abar_im = mag * jnp.sin(ang)
    den = a_re * a_re + a_im * a_im
    nr = abar_re - 1.0
    ni = abar_im
    f_re = (nr * a_re + ni * a_im) / den
    f_im = (ni * a_re - nr * a_im) / den
    b_re = b_re.astype(f32)
    b_im = b_im.astype(f32)
    bbar_re = f_re[..., None] * b_re - f_im[..., None] * b_im
    bbar_im = f_re[..., None] * b_im + f_im[..., None] * b_re
    bu_re = jnp.einsum('bsgp,gnp->bsgn', uf, bbar_re)
    bu_im = jnp.einsum('bsgp,gnp->bsgn', uf, bbar_im)
    a_seq_re = jnp.broadcast_to(abar_re, bu_re.shape)
    a_seq_im = jnp.broadcast_to(abar_im, bu_im.shape)
    _, _, h_re, h_im = lax.associative_scan(
        _complex_linear_combine, (a_seq_re, a_seq_im, bu_re, bu_im), axis=1)
    y = (jnp.einsum('bsgn,gpn->bsgp', h_re, c_re.astype(f32))
         - jnp.einsum('bsgn,gpn->bsgp', h_im, c_im.astype(f32))
         + d_skip.astype(f32) * uf)
    y = y.reshape(bsz, seq, SSM_WIDTH)
    g = jax.nn.gelu(y)
    out = g * jax.nn.sigmoid(g @ w_glu.astype(f32) + b_glu.astype(f32))
    return out.astype(u.dtype)


def conv_module(v, w_dw, b_dw, ln_g, ln_b, w_proj):
    h = v[..., :CONV_WIDTH] * jax.nn.sigmoid(v[..., CONV_WIDTH:])
    h = jnp.pad(h, ((0, 0), (CONV_KERNEL - 1, 0), (0, 0)))
    h = lax.conv_general_dilated(h, w_dw, window_strides=(1,), padding='VALID',
                                 dimension_numbers=('NWC', 'WIO', 'NWC'),
                                 feature_group_count=CONV_WIDTH) + b_dw
    h = jax.nn.silu(layer_norm(h, ln_g, ln_b))
    return h @ w_proj


def pool_mixer(u, w_group, scale, w_proj):
    bsz, seq, _ = u.shape
    uf = u.astype(jnp.float32).reshape(bsz, seq, len(POOL_WINDOWS), POOL_GROUP)
    cs = jnp.cumsum(uf, axis=1)
    pos = jnp.arange(1, seq + 1, dtype=jnp.float32)
    outs = []
    for k, w in enumerate(POOL_WINDOWS):
        c = cs[:, :, k]
        lagged = jnp.pad(c, ((0, 0), (w, 0), (0, 0)))[:, :seq]
        mean = (c - lagged) / jnp.minimum(pos, float(w))[None, :, None]
        outs.append(mean - uf[:, :, k])
    p = jnp.stack(outs, axis=2)
    p = jnp.einsum('bsgc,gcd->bsgd', p, w_group.astype(jnp.float32))
    p = p.reshape(bsz, seq, POOL_WIDTH) * scale.astype(jnp.float32)
    return p.astype(u.dtype) @ w_proj


def hybrid_layer(x, norm1, w_in, b_gate, a_re, a_im, log_dt, b_re, b_im, c_re, c_im,
                 d_skip, w_glu, b_glu, ssm_w_proj, conv_w_dw, conv_b_dw, conv_ln_g,
                 conv_ln_b, conv_w_proj, pool_w_group, pool_scale, pool_w_proj, w_out,
                 norm2, w_gate, w_up, w_down):
    bsz, seq, _ = x.shape
    h = rms_norm(x, norm1)
    z = h @ w_in
    o1 = SSM_WIDTH
    o2 = o1 + 2 * CONV_WIDTH
    o3 = o2 + POOL_WIDTH
    u_a = z[..., :o1]
    v_b = z[..., o1:o2]
    u_c = z[..., o2:o3]
    gates = jax.nn.sigmoid(z[..., o3:] + b_gate).reshape(bsz, seq, N_BRANCHES, D_MODEL)
    y_a = s5_mixer(u_a, a_re, a_im, log_dt, b_re, b_im, c_re, c_im, d_skip, w_glu, b_glu) @ ssm_w_proj
    y_b = conv_module(v_b, conv_w_dw, conv_b_dw, conv_ln_g, conv_ln_b, conv_w_proj)
    y_c = pool_mixer(u_c, pool_w_group, pool_scale, pool_w_proj)
    merged = gates[:, :, 0] * y_a + gates[:, :, 1] * y_b + gates[:, :, 2] * y_c
    x = x + merged @ w_out
    h = rms_norm(x, norm2)
    x = x + (jax.nn.silu(h @ w_gate) * (h @ w_up)) @ w_down
    return x


def setup_inputs(seed: int = 0) -> dict:
    key = jax.random.key(seed)
    ks = iter(jax.random.split(key, 40))
    f32 = jnp.float32

    def nrm(shape, scale):
        return jax.random.normal(next(ks), shape, f32) * scale

    L, D, G, N, P = DEPTH, D_MODEL, SSM_GROUPS, SSM_STATE, SSM_GROUP
    n_idx = jnp.arange(N, dtype=f32)
    inputs = {
        'x': nrm((BATCH, SEQ, D), 1.0),
        'norm1': 1.0 + nrm((L, D), 0.02),
        'w_in': nrm((L, D, IN_WIDTH), D ** -0.5),
        'b_gate': nrm((L, N_BRANCHES * D), 0.01),
        'ssm_a_re': -0.5 + nrm((L, G, N), 0.01),
        'ssm_a_im': math.pi * n_idx + nrm((L, G, N), 0.01),
        'ssm_log_dt': jax.random.uniform(next(ks), (L, G), f32,
                                         math.log(DT_MIN), math.log(DT_MAX)),
        'ssm_b_re': nrm((L, G, N, P), (2.0 * P) ** -0.5),
        'ssm_b_im': nrm((L, G, N, P), (2.0 * P) ** -0.5),
        'ssm_c_re': nrm((L, G, P, N), (2.0 * N) ** -0.5 * 4.0),
        'ssm_c_im': nrm((L, G, P, N), (2.0 * N) ** -0.5 * 4.0),
        'ssm_d': nrm((L, G, P), 1.0),
        'ssm_w_glu': nrm((L, SSM_WIDTH, SSM_WIDTH), SSM_WIDTH ** -0.5),
        'ssm_b_glu': nrm((L, SSM_WIDTH), 0.01),
        'ssm_w_proj': nrm((L, SSM_WIDTH, D), SSM_WIDTH ** -0.5),
        'conv_w_dw': nrm((L, CONV_KERNEL, 1, CONV_WIDTH), CONV_KERNEL ** -0.5),
        'conv_b_dw': nrm((L, CONV_WIDTH), 0.01),
        'conv_ln_g': 1.0 + nrm((L, CONV_WIDTH), 0.02),
        'conv_ln_b': nrm((L, CONV_WIDTH), 0.01),
        'conv_w_proj': nrm((L, CONV_WIDTH, D), CONV_WIDTH ** -0.5),
        'pool_w_group': nrm((L, len(POOL_WINDOWS), POOL_GROUP, POOL_GROUP), POOL_GROUP ** -0.5),
        'pool_scale': 1.0 + nrm((L, POOL_WIDTH), 0.02),
        'pool_w_proj': nrm((L, POOL_WIDTH, D), POOL_WIDTH ** -0.5),
        'w_out': nrm((L, D, D), D ** -0.5),
        'norm2': 1.0 + nrm((L, D), 0.02),
        'ffn_w_gate': nrm((L, D, FFN_HIDDEN), D ** -0.5),
        'ffn_w_up': nrm((L, D, FFN_HIDDEN), D ** -0.5),
        'ffn_w_down': nrm((L, FFN_HIDDEN, D), FFN_HIDDEN ** -0.5),
        'final_norm': 1.0 + nrm((D,), 0.02),
    }
    return inputs


def reference(x, norm1, w_in, b_gate, ssm_a_re, ssm_a_im, ssm_log_dt, ssm_b_re, ssm_b_im,
              ssm_c_re, ssm_c_im, ssm_d, ssm_w_glu, ssm_b_glu, ssm_w_proj, conv_w_dw,
              conv_b_dw, conv_ln_g, conv_ln_b, conv_w_proj, pool_w_group, pool_scale,
              pool_w_proj, w_out, norm2, ffn_w_gate, ffn_w_up, ffn_w_down, final_norm):
    for l in range(DEPTH):
        x = hybrid_layer(x, norm1[l], w_in[l], b_gate[l], ssm_a_re[l], ssm_a_im[l],
                         ssm_log_dt[l], ssm_b_re[l], ssm_b_im[l], ssm_c_re[l], ssm_c_im[l],
                         ssm_d[l], ssm_w_glu[l], ssm_b_glu[l], ssm_w_proj[l], conv_w_dw[l],
                         conv_b_dw[l], conv_ln_g[l], conv_ln_b[l], conv_w_proj[l],
                         pool_w_group[l], pool_scale[l], pool_w_proj[l], w_out[l], norm2[l],
                         ffn_w_gate[l], ffn_w_up[l], ffn_w_down[l])
    return rms_norm(x, final_norm)
```

```python
import bisect
import numpy as np
import concourse.bass as bass
import concourse.mybir as mybir
from concourse.bass_utils import run_bass_kernel_spmd

F32 = mybir.dt.float32
BF16 = mybir.dt.bfloat16
AF = mybir.ActivationFunctionType
ALU = mybir.AluOpType
MAGIC = 12582912.0
TWO_PI = 6.283185307179586

NL = 2
T = 2048
NTB = 4
KT = 8
NF = 22
LC = 112
NCOL = NL * LC + 8
NSLOT_W = 6
SLOTB = 4096
NDMASEM = 8
HOIST = True
TAIL_NORM = True
DEBUG = False
SAME_SYNC = True
DBG = {}
STOP = 0
MMTEST = False
KTEST = False
NOCARRY = False


class _Stop(Exception):
    pass


def chk(n):
    if STOP == n:
        raise _Stop()


class V:
    __slots__ = ("ap", "sp", "lo", "hi")

    def __init__(self, ap, sp, lo, hi):
        self.ap, self.sp, self.lo, self.hi = ap, sp, lo, hi

    def w(self, ap):
        return V(ap, self.sp, self.lo, self.hi)


class Tile:
    def __init__(self, ap, sp, off, dims, esz):
        self.ap, self.sp, self.off, self.dims, self.esz = ap, sp, off, list(dims), esz
        st, acc = [], 1
        for d in reversed(self.dims):
            st.append(acc)
            acc *= d
        self.st = list(reversed(st))

    def __getitem__(self, idx):
        if not isinstance(idx, tuple):
            idx = (idx,)
        free = idx[1:]
        lo = hi = 0
        for i, d in enumerate(self.dims):
            if i < len(free):
                ix = free[i]
                if isinstance(ix, int):
                    a = b = ix
                else:
                    stp = ix.step or 1
                    a = ix.start or 0
                    stop = d if ix.stop is None else ix.stop
                    cnt = (stop - a + stp - 1) // stp
                    b = a + (cnt - 1) * stp
            else:
                a, b = 0, d - 1
            lo += a * self.st[i]
            hi += b * self.st[i]
        if self.sp == "P":
            return V(self.ap[idx], self.sp, self.off, self.off + 2048)
        return V(self.ap[idx], self.sp, self.off + lo * self.esz, self.off + (hi + 1) * self.esz)


class Track:
    def __init__(self):
        self.los = []
        self.recs = []

    def access(self, lo, hi, op, eng, is_dma, is_write):
        deps = set()
        i = bisect.bisect_right(self.los, lo) - 1
        if i < 0:
            i = 0
        new = []
        j = i
        cur = lo
        while j < len(self.recs) and self.recs[j][0] < hi:
            r = self.recs[j]
            if r[1] <= lo:
                j += 1
                i = j
                continue
            if r[2] is not None:
                deps.add(r[2])
            if is_write:
                deps.update(r[3].values())
                deps.update(r[4])
            if r[0] < lo:
                new.append([r[0], lo, r[2], dict(r[3]), list(r[4])])
            a, b = max(r[0], lo), min(r[1], hi)
            if not is_write:
                if cur < a:
                    new.append([cur, a, None, ({} if is_dma else {eng: op}), ([op] if is_dma else [])])
                rd = dict(r[3])
                dl = list(r[4])
                if is_dma:
                    dl.append(op)
                else:
                    rd[eng] = op
                new.append([a, b, r[2], rd, dl])
            cur = b
            if r[1] > hi:
                new.append([hi, r[1], r[2], dict(r[3]), list(r[4])])
            j += 1
        if is_write:
            lead = [x for x in new if x[1] <= lo]
            trail = [x for x in new if x[0] >= hi]
            new = lead + [[lo, hi, op, {}, []]] + trail
        else:
            if cur < hi:
                new.append([cur, hi, None, ({} if is_dma else {eng: op}), ([op] if is_dma else [])])
        self.recs[i:j] = new
        self.los[i:j] = [x[0] for x in new]
        return deps


class Prog:
    def __init__(self):
        self.ops = []

    def add(self, eng, fn, outs=(), ins=(), dma=False, final=False):
        self.ops.append((eng, fn, list(outs), list(ins), dma, final))


def build_program():
    nc = bass.Bass("TRN2", target_bir_lowering=False)

    def din(name, shape):
        return nc.dram_tensor(name, list(shape), F32, kind="ExternalInput").ap()

    d_x = din("xT", [128, NTB, KT, 512])
    d_cols = din("cols", [128, NCOL])
    d_ssmw = din("ssmw", [NL, 128, 5, 4, 128])
    d_ssmc = din("ssmc", [NL, 128, 16, 2, 32])
    d_ssmb = din("ssmb", [NL, 128, 16, 2, 32])
    d_win = din("w_in", [NL, 1024, 5120])
    d_wglu = din("w_glu", [NL, 512, 512])
    d_wpa = din("w_pa", [NL, 512, 1024])
    d_wpb = din("w_pb", [NL, 512, 1024])
    d_wpc = din("w_pc", [NL, 512, 1024])
    d_convd = din("convd", [NL, 4, 128, 31, 128])
    d_wgrp = din("w_grp", [NL, 4, 128, 128])
    d_wout = din("w_out", [NL, 1024, 1024])
    d_wg = din("w_g", [NL, 1024, 2816])
    d_wu = din("w_u", [NL, 1024, 2816])
    d_wd = din("w_d", [NL, 2816, 1024])
    d_out = nc.dram_tensor("outT", [128, NTB, KT, 512], F32, kind="ExternalOutput").ap()

    o_X = 0
    o_H = o_X + 65536
    o_UA = o_H + 32768
    o_UC = o_UA + 16384
    o_HC = o_UC + 4 * 2064 * 2
    o_CVM = o_HC + 4 * 2078 * 2
    o_SL = o_CVM + 16384
    o_MISC = o_SL + NSLOT_W * SLOTB
    o_CONST = o_MISC + 16384
    TOTAL = o_CONST + 4932
    assert o_CVM % 4 == 0

    P = Prog()
    ctx = {}

    def body(RAW, PS):
        def mk(off, dims, dt):
            esz = 4 if dt == F32 else 2
            n = int(np.prod(dims))
            nb = n * esz
            assert off % 4 == 0 and nb % 4 == 0, (off, dims)
            a = RAW[:, off // 4:(off + nb) // 4]
            if dt != F32:
                a = a.bitcast(dt)
            if len(dims) == 2:
                a = a.rearrange("p (a b) -> p a b", a=dims[0])
            elif len(dims) == 3:
                a = a.rearrange("p (a b c) -> p a b c", a=dims[0], b=dims[1])
            elif len(dims) == 4:
                a = a.rearrange("p (a b c d) -> p a b c d", a=dims[0], b=dims[1], c=dims[2])
            return Tile(a, "S", off, dims, esz)

        X = mk(o_X, [NTB, KT, 512], F32)
        H = mk(o_H, [NTB, KT, 512], BF16)
        UA = mk(o_UA, [4, 2048], BF16)
        UC = mk(o_UC, [4, 2064], BF16)
        HC = mk(o_HC, [4, 2078], BF16)
        CV = mk(o_CVM, [2, 4, 512], F32)
        MERGED = mk(o_CVM, [2, 8, 512], BF16)
        ACTB = mk(o_UA, [NTB, 8, 512], BF16)
        COLS = mk(o_CONST, [NCOL], F32)
        ONESB = mk(o_CONST + 928, [128], BF16)
        ONESF = mk(o_CONST + 1184, [128], F32)
        INVT = mk(o_CONST + 1696, [4, 16], F32)
        IDX = mk(o_CONST + 1952, [512], F32)
        EPSC = mk(o_CONST + 4000, [1], F32)
        SSTATE = mk(o_CONST + 4004, [14, 16], F32)
        K7R = mk(o_CONST + 4900, [8], F32)
        banks = [Tile(PS[:, b * 512:(b + 1) * 512], "P", b * 2048, [512], 4) for b in range(8)]
        st = {"bank": 0, "slot": 0}

        def nb():
            b = banks[st["bank"] % 8]
            st["bank"] += 1
            return b

        def col(i):
            return COLS[:, i:i + 1]

        def tt(eng, out, a, b, op):
            P.add(eng, lambda e: e.tensor_tensor(out=out.ap, in0=a.ap, in1=b.ap, op=op), [out], [a, b])

        def ts(eng, out, a, s1, op0, s2=None, op1=None):
            ins = [a] + [s for s in (s1, s2) if isinstance(s, V)]
            s1a = s1.ap if isinstance(s1, V) else s1
            s2a = s2.ap if isinstance(s2, V) else s2
            if op1 is None:
                P.add(eng, lambda e: e.tensor_scalar(out=out.ap, in0=a.ap, scalar1=s1a, scalar2=None, op0=op0), [out], ins)
            else:
                P.add(eng, lambda e: e.tensor_scalar(out=out.ap, in0=a.ap, scalar1=s1a, scalar2=s2a, op0=op0, op1=op1), [out], ins)

        def stt(out, a, s, b, op0, op1):
            ins = [a, b] + ([s] if isinstance(s, V) else [])
            sa = s.ap if isinstance(s, V) else s
            P.add("dve", lambda e: e.scalar_tensor_tensor(out=out.ap, in0=a.ap, scalar=sa, in1=b.ap, op0=op0, op1=op1), [out], ins)

        def act(out, a, func, bias=None, scale=None):
            ins = [a] + [s for s in (bias, scale) if isinstance(s, V)]
            kw = {}
            if bias is not None:
                kw["bias"] = bias.ap if isinstance(bias, V) else bias
            if scale is not None:
                kw["scale"] = scale.ap if isinstance(scale, V) else scale
            P.add("act", lambda e: e.activation(out=out.ap, in_=a.ap, func=func, **kw), [out], ins)

        def memset(eng, out, val):
            P.add(eng, lambda e: e.memset(out.ap, val), [out], [])

        def mm(out, pairs, tile_position=None, start=True):
            def fn(pe):
                n = len(pairs)
                ins = None
                for i, (l, r) in enumerate(pairs):
                    kw = {}
                    if tile_position is not None:
                        kw["tile_position"] = tile_position
                    ins = pe.matmul(out.ap, lhsT=l.ap, rhs=r.ap, start=(start and i == 0), stop=(i == n - 1), **kw)
                return ins
            P.add("pe", fn, [out], [v for p in pairs for v in p])

        def wload(parts):
            s = st["slot"] % NSLOT_W
            st["slot"] += 1
            off = o_SL + s * SLOTB
            tiles = []
            for src, dims in parts:
                t_ = mk(off, dims, BF16)
                off += int(np.prod(dims)) * 2
                assert off <= o_SL + (s + 1) * SLOTB
                P.add("pool", (lambda e, t_=t_, src=src: e.dma_start(out=t_.ap, in_=src)), [t_[:]], [], dma=True)
                tiles.append(t_)
            return tiles

        def sload(tile, src):
            P.add("sp", lambda e: e.dma_start(out=tile.ap, in_=src), [tile], [], dma=True)

        def dump(name, v, shape, dt):
            if not DEBUG:
                return
            d = nc.dram_tensor("dbg_" + name, [128] + list(shape), dt, kind="ExternalOutput").ap()
            DBG[name] = (shape, dt)
            P.add("sp", lambda e: e.dma_start(out=d, in_=v.ap), [], [v], dma=True, final=True)

        ctx["dump"] = dump

        def wrap(eng, x, tmp):
            ts(eng, tmp, x, MAGIC, ALU.add, MAGIC, ALU.subtract)
            tt(eng, x, x, tmp, ALU.subtract)

        sload(COLS[:], d_cols[:, :])
        memset("dve", ONESB[:], 1.0 / 1024.0)
        memset("dve", ONESF[:], 1.0 / 512.0)
        memset("dve", EPSC[:], 1e-6)
        for k in range(4):
            w = 2 ** (k + 1)
            memset("dve", INVT[:, k, :], 1.0 / w)
            for t_ in range(w - 1):
                memset("dve", INVT[:, k, t_:t_ + 1], 1.0 / (t_ + 1))
        memset("dve", IDX[:, 0:1], 0.0)
        n = 1
        while n < 512:
            ts("dve", IDX[:, n:2 * n], IDX[:, 0:n], float(n), ALU.add)
            n *= 2
        ts("dve", K7R[:], IDX[:, 0:8], -1.0, ALU.mult, 7.0, ALU.add)
        for tb in range(NTB):
            sload(X[:, tb], d_x[:, tb])
        ctx["pre_gen"] = True

        SQ = mk(o_MISC, [KT, 512], BF16)
        RS = mk(o_MISC + 8192, [2, 512], F32)

        def rmsnorm(tb, gbase, dst, inplace=False):
            act(SQ[:], X[:, tb], AF.Square)
            bk = nb()
            mm(bk[:], [(ONESB[:], SQ[:, kt]) for kt in range(KT)])
            rs = RS[:, tb % 2]
            act(rs, bk[:], AF.Sqrt, bias=EPSC[:], scale=1.0)
            P.add("dve", lambda e: e.reciprocal(out=rs.ap, in_=rs.ap), [rs], [rs])
            for kt in range(KT):
                stt(dst[:, tb, kt], X[:, tb, kt], col(gbase + kt), rs, ALU.mult, ALU.mult)

        so = o_UC
        hb = o_HC
        W1PAD = mk(so, [4, 8, 2, 128], BF16)
        PRMC = mk(hb, [16, 2, 32], F32)
        FBW = mk(hb + 4096, [2, 4, 128], F32)
        DFW = mk(hb + 8192, [2, 4, 128], F32)
        FBT = mk(hb + 12288, [16, 2, 32], BF16)
        PK = mk(hb + 14336, [2, 16, 9], F32)
        KMAT = mk(hb + 15616, [8, 128], BF16)
        T1o = hb + 17664
        PRMW = mk(T1o, [5, 4, 128], F32)
        PRMB = mk(hb + 27904, [16, 2, 32], F32)
        assert hb + 32000 <= o_SL
        G = [mk(o_MISC + 4096 + i * 2048, [4, 128], F32) for i in range(6)]
        WC = mk(o_MISC, [4, 2, 9, 32], BF16)
        YS = [mk(o_MISC + 4608, [2, 256], F32), mk(hb + 27904, [2, 256], F32)]
        HPREV = mk(o_MISC + 6656, [4, 2, 256], BF16)
        UD = mk(o_MISC + 10752, [8, 256], BF16)

        def bc(v, axis, shape):
            return v.w(v.ap.unsqueeze(axis).to_broadcast(shape))

        def ps_(v, a, b):
            return v.w(v.ap[a:b])

        def ssm_layer_gen(l):
            c_ss = l * LC + 64
            sload(PRMW[:], d_ssmw[l])
            sload(PRMC[:], d_ssmc[l])
            sload(PRMB[:], d_ssmb[l])
            aRe, aIm, lgd, bRe, bIm = (PRMW[:, i] for i in range(5))
            dtw, mag, sn, cs, t0, t1 = (G[i][:] for i in range(6))
            drew, frw = DFW[:, 0], DFW[:, 1]
            act(dtw, lgd, AF.Exp)
            tt("dve", drew, dtw, aRe, ALU.mult)
            stt(frw, aIm, 1.0 / TWO_PI, dtw, ALU.mult, ALU.mult)
            wrap("dve", frw, t0)
            act(mag, drew, AF.Exp)
            act(sn, frw, AF.Sin, scale=TWO_PI)
            ts("dve", t1, frw, 0.25, ALU.add)
            wrap("dve", t1, t0)
            act(cs, t1, AF.Sin, scale=TWO_PI)
            tt("dve", DFW[:, 0], mag, cs, ALU.mult)
            tt("dve", DFW[:, 1], mag, sn, ALU.mult)
            ts("dve", cs, DFW[:, 0], -1.0, ALU.add)
            sn = DFW[:, 1]
            tt("dve", mag, aRe, aRe, ALU.mult)
            tt("dve", t0, aIm, aIm, ALU.mult)
            tt("dve", mag, mag, t0, ALU.add)
            P.add("dve", lambda e: e.reciprocal(out=mag.ap, in_=mag.ap), [mag], [mag])
            fre_w, fim_w = G[0][:], G[2][:]
            tt("dve", t0, cs, aRe, ALU.mult)
            tt("dve", t1, sn, aIm, ALU.mult)
            tt("dve", t0, t0, t1, ALU.add)
            tt("dve", fre_w, t0, mag, ALU.mult)
            tt("dve", t0, sn, aRe, ALU.mult)
            tt("dve", t1, cs, aIm, ALU.mult)
            tt("dve", t0, t0, t1, ALU.subtract)
            tt("dve", fim_w, t0, mag, ALU.mult)
            tt("dve", t0, fre_w, bRe, ALU.mult)
            tt("dve", t1, fim_w, bIm, ALU.mult)
            tt("dve", FBW[:, 0], t0, t1, ALU.subtract)
            tt("dve", t0, fre_w, bIm, ALU.mult)
            tt("dve", t1, fim_w, bRe, ALU.mult)
            tt("dve", FBW[:, 1], t0, t1, ALU.add)
            sA = lambda i: SSTATE[:, i]
            S_dt, S_dre, S_fr, S_R8, S_fr8, S_fre, S_fim = (sA(i) for i in range(7))
            u = [sA(i) for i in range(7, 13)]
            aRe_s, aIm_s, lgd_s = COLS[:, c_ss:c_ss + 16], COLS[:, c_ss + 16:c_ss + 32], COLS[:, c_ss + 32:c_ss + 48]
            act(S_dt, lgd_s, AF.Exp)
            tt("dve", S_dre, S_dt, aRe_s, ALU.mult)
            stt(S_fr, aIm_s, 1.0 / TWO_PI, S_dt, ALU.mult, ALU.mult)
            wrap("dve", S_fr, u[0])
            act(u[1], S_dre, AF.Exp)
            act(u[2], S_fr, AF.Sin, scale=TWO_PI)
            ts("dve", u[3], S_fr, 0.25, ALU.add)
            wrap("dve", u[3], u[0])
            act(u[3], u[3], AF.Sin, scale=TWO_PI)
            tt("dve", u[3], u[1], u[3], ALU.mult)
            ts("dve", u[3], u[3], -1.0, ALU.add)
            tt("dve", u[2], u[1], u[2], ALU.mult)
            tt("dve", u[1], aRe_s, aRe_s, ALU.mult)
            tt("dve", u[0], aIm_s, aIm_s, ALU.mult)
            tt("dve", u[1], u[1], u[0], ALU.add)
            P.add("dve", lambda e: e.reciprocal(out=u[1].ap, in_=u[1].ap), [u[1]], [u[1]])
            tt("dve", u[0], u[3], aRe_s, ALU.mult)
            tt("dve", u[4], u[2], aIm_s, ALU.mult)
            tt("dve", u[0], u[0], u[4], ALU.add)
            tt("dve", S_fre, u[0], u[1], ALU.mult)
            tt("dve", u[0], u[2], aRe_s, ALU.mult)
            tt("dve", u[4], u[3], aIm_s, ALU.mult)
            tt("dve", u[0], u[0], u[4], ALU.subtract)
            tt("dve", S_fim, u[0], u[1], ALU.mult)
            ts("dve", u[0], S_dre, 8.0, ALU.mult)
            act(S_R8, u[0], AF.Exp)
            ts("dve", S_fr8, S_fr, 8.0, ALU.mult)
            wrap("dve", S_fr8, u[0])
            B0, B1 = PRMB[:, :, 0], PRMB[:, :, 1]
            fr_b = bc(S_fre, 2, [128, 16, 32])
            fi_b = bc(S_fim, 2, [128, 16, 32])
            X0 = mk(T1o + 2048, [16, 32], F32)
            X1 = mk(T1o, [16, 32], F32)
            tt("dve", X0[:], fr_b, B0, ALU.mult)
            tt("dve", X1[:], fi_b, B1, ALU.mult)
            tt("dve", FBT[:, :, 0], X0[:], X1[:], ALU.subtract)
            tt("dve", X0[:], fr_b, B1, ALU.mult)
            tt("dve", X1[:], fi_b, B0, ALU.mult)
            tt("dve", FBT[:, :, 1], X0[:], X1[:], ALU.add)
            PHk = mk(T1o + 4096, [16, 9], F32)
            PHc = mk(T1o + 4096 + 576, [16, 9], F32)
            RNk = mk(T1o + 4096 + 1152, [16, 9], F32)
            MGk = mk(T1o + 4096 + 1728, [16, 9], F32)
            k9 = bc(IDX[:, 0:9], 1, [128, 16, 9])
            tt("dve", PHk[:], bc(S_fr, 2, [128, 16, 9]), k9, ALU.mult)
            wrap("dve", PHk[:], RNk[:])
            ts("dve", PHc[:], PHk[:], 0.25, ALU.add)
            wrap("dve", PHc[:], RNk[:])
            tt("dve", MGk[:], bc(S_dre, 2, [128, 16, 9]), k9, ALU.mult)
            act(MGk[:], MGk[:], AF.Exp)
            act(PHk[:], PHk[:], AF.Sin, scale=TWO_PI)
            act(PHc[:], PHc[:], AF.Sin, scale=TWO_PI)
            tt("dve", PK[:, 0], MGk[:], PHc[:], ALU.mult)
            tt("dve", PK[:, 1], MGk[:], PHk[:], ALU.mult)

        def capture(fn):
            keep = P.ops
            P.ops = []
            fn()
            got = P.ops
            P.ops = keep
            return got

        def layer(l):
            cb = l * LC
            c_g1, c_g2, c_bg, c_bglu, c_cb, c_lng, c_lnb, c_psc, c_dsk, c_ss = (
                cb, cb + 8, cb + 16, cb + 40, cb + 44, cb + 48, cb + 52, cb + 56, cb + 60, cb + 64)
            win = d_win[l].rearrange("(kt p) n -> p kt n", p=128)

            if l == 0 or not TAIL_NORM:
                for tb in range(NTB):
                    rmsnorm(tb, c_g1, H)
            if l == 0:
                dump("h1", H[:], [NTB, KT, 512], BF16)

            if not HOIST:
                ssm_layer_gen(l)
            memset("dve", W1PAD[:], 0.0)
            memset("dve", KMAT[:], 0.0)
            memset("dve", HPREV[:], 0.0)

            chk(1)
            def gen_w1(pt):
                W1F = mk(T1o, [8, 2, 128], F32)
                P1 = mk(T1o + 8192, [2, 128], F32)
                P2 = mk(T1o + 9216, [2, 128], F32)
                ar_b = bc(DFW[:, 0, pt], 1, [128, 2, 128])
                ai_b = bc(DFW[:, 1, pt], 1, [128, 2, 128])
                P.add("dve", (lambda e, pt=pt: e.tensor_copy(out=W1F[:, 7].ap, in_=FBW[:, :, pt].ap)), [W1F[:, 7]], [FBW[:, :, pt]])
                ar1, ai1 = DFW[:, 0, pt], DFW[:, 1, pt]
                for sg_ in range(6, -1, -1):
                    wre, wim = W1F[:, sg_ + 1, 0], W1F[:, sg_ + 1, 1]
                    tt("dve", P1[:, 0], ar1, wre, ALU.mult)
                    tt("dve", P2[:, 0], ai1, wre, ALU.mult)
                    tt("dve", P2[:, 1], ai1, wim, ALU.mult)
                    tt("dve", P1[:, 1], ar1, wim, ALU.mult)
                    tt("dve", W1F[:, sg_, 0], P1[:, 0], P2[:, 1], ALU.subtract)
                    tt("dve", W1F[:, sg_, 1], P1[:, 1], P2[:, 0], ALU.add)
                for q4 in range(4):
                    dst = W1PAD[32 * q4:32 * q4 + 32, q4]
                    src = ps_(W1F[:], 32 * q4, 32 * q4 + 32)
                    src = src.w(src.ap.rearrange("p a b c -> p (a b c)"))
                    dstf = dst.w(dst.ap.rearrange("p a b c -> p (a b c)"))
                    P.add("pool", (lambda e, dstf=dstf, src=src: e.dma_start(out=dstf.ap, in_=src.ap)), [dstf], [src], dma=True)

            def gen_wck(pt):
                chk(2)
                tA = mk(T1o, [4, 9, 32], F32)[:]
                tB = mk(T1o + 4608, [4, 9, 32], F32)[:]
                pkr = bc(PK[:, 0, 4 * pt:4 * pt + 4, :], 3, [128, 4, 9, 32])
                pki = bc(PK[:, 1, 4 * pt:4 * pt + 4, :], 3, [128, 4, 9, 32])
                cr = bc(PRMC[:, 4 * pt:4 * pt + 4, 0, :], 2, [128, 4, 9, 32])
                ci = bc(PRMC[:, 4 * pt:4 * pt + 4, 1, :], 2, [128, 4, 9, 32])
                tt("dve", tA, pkr, cr, ALU.mult)
                tt("dve", tB, pki, ci, ALU.mult)
                tt("dve", WC[:, :, 0, :, :], tA, tB, ALU.subtract)
                tt("dve", tA, pkr, ci, ALU.mult)
                tt("dve", tB, pki, cr, ALU.mult)
                stt(WC[:, :, 1, :, :], tA, -1.0, tB, ALU.mult, ALU.subtract)

            def gen_kmat(pt):
                kb = nb()
                for q4 in range(4):
                    q = 4 * pt + q4
                    mm(kb[32 * q4:32 * q4 + 32, 0:256],
                       [(FBT[:, q, 0], WC[:, q4, 0, 0:8, :]), (FBT[:, q, 1], WC[:, q4, 1, 0:8, :])], tile_position=(0, 32 * q4))
                for q4 in range(4):
                    src = kb[32 * q4:32 * q4 + 32, 0:256]
                    src = src.w(src.ap.rearrange("p (k j) -> p k j", k=8))
                    dst = KMAT[32 * q4:32 * q4 + 32, :, 32 * q4:32 * q4 + 32]
                    act(dst, src, AF.Copy)

            gen_w1(0)
            gen_wck(0)
            for c in range(2):
                (W,) = wload([(win[:, :, c * 256:(c + 1) * 256], [KT, 256])])
                for tb in range(NTB):
                    for mm_ in range(2):
                        m = 2 * c + mm_
                        bk = nb()
                        mm(bk[:], [(W[:, kt, mm_ * 128:(mm_ + 1) * 128], H[:, tb, kt]) for kt in range(KT)])
                        act(UA[:, m, tb * 512:(tb + 1) * 512], bk[:], AF.Copy)
            if l == 0:
                dump("ua", UA[:], [4, 2048], BF16)

            for pt in range(4):
                if pt > 0:
                    gen_wck(pt)
                gen_kmat(pt)
                chk(4)
                uvw = UA[:, pt]
                udsrc = uvw.w(uvw.ap.rearrange("p (c s) -> p s c", s=8))
                P.add("dve", (lambda e, udsrc=udsrc: e.tensor_copy(out=UD[:].ap, in_=udsrc.ap)), [UD[:]], [udsrc])
                sbs = []
                for q4 in range(4):
                    sb_ = nb()
                    for ri in range(2):
                        mm(sb_[:, ri * 256:(ri + 1) * 256], [(W1PAD[:, q4, sg_, ri, :], UD[:, sg_]) for sg_ in range(8)])
                    sbs.append(sb_)
                nm = ["rn", "t1", "t2", "bre", "bim", "gre", "gim", "ph0", "phc0"]
                L2 = {n_: mk(T1o + i * 1024, [256], F32)[:] for i, n_ in enumerate(nm)}
                rn, t1_, t2_, bre, bim, gre, gim = (L2[n_] for n_ in nm[:7])
                t3_ = mk(o_MISC + 14848, [256], F32)[:]
                t4_ = rn
                tabs = [(L2["ph0"], L2["phc0"]),
                        (mk(hb + 27904 + 2048, [256], F32)[:], mk(hb + 27904 + 3072, [256], F32)[:])]

                def gen_tables(q4):
                    q = 4 * pt + q4
                    ph, phc = tabs[q4 % 2]
                    ts("dve", ph, IDX[:, 1:257], SSTATE[:, 4, q:q + 1], ALU.mult)
                    wrap("dve", ph, rn)
                    ts("dve", rn, ph, 0.25, ALU.is_gt)
                    stt(phc, ph, 0.25, rn, ALU.add, ALU.subtract)
                    act(ph, ph, AF.Sin, scale=TWO_PI)
                    act(phc, phc, AF.Sin, scale=TWO_PI)

                gen_tables(0)
                for q4 in range(4):
                    q = 4 * pt + q4
                    if q4 + 1 < 4:
                        gen_tables(q4 + 1)
                    ph, phc = tabs[q4 % 2]
                    sb_ = sbs[q4]
                    sre, sim = sb_[:, 0:256], sb_[:, 256:512]
                    tt("dve", t1_, phc, sre, ALU.mult)
                    tt("dve", t2_, ph, sim, ALU.mult)
                    tt("dve", t3_, phc, sim, ALU.mult)
                    tt("dve", t4_, ph, sre, ALU.mult)
                    tt("dve", bre, t1_, t2_, ALU.add)
                    tt("dve", bim, t3_, t4_, ALU.subtract)
                    Rb = SSTATE[:, 3, q:q + 1]
                    for (gg, bb_) in ((gre, bre), (gim, bim)):
                        P.add("dve", (lambda e, gg=gg, bb_=bb_, Rb=Rb: e.tensor_tensor_scan(
                            out=gg.ap, data0=Rb.ap.to_broadcast([128, 256]), data1=bb_.ap, initial=0.0,
                            op0=ALU.mult, op1=ALU.add)), [gg], [bb_, Rb])
                    tt("dve", t1_, phc, gre, ALU.mult)
                    tt("dve", t2_, ph, gim, ALU.mult)
                    tt("dve", t3_, phc, gim, ALU.mult)
                    tt("dve", t4_, ph, gre, ALU.mult)
                    tt("dve", HPREV[:, q4, 0, 0:256], t1_, t2_, ALU.subtract)
                    tt("dve", HPREV[:, q4, 1, 0:256], t3_, t4_, ALU.add)
                if pt + 1 < 4:
                    gen_w1(pt + 1)
                chk(5)
                ybs = [nb() for _ in range(4)]
                for tau in range(8):
                    yo = ybs[tau // 2][:, (tau % 2) * 256:(tau % 2 + 1) * 256]
                    rd = [KMAT[:, 0:tau + 1], UD[:], WC[:], HPREV[:]]

                    def fn(pe, tau=tau, yo=yo, pt=pt):
                        ins = None
                        for sg_ in range(tau + 1):
                            ins = pe.matmul(yo.ap, lhsT=(W1PAD[:, 0, tau - sg_, 0, :].ap if KTEST else KMAT[:, tau - sg_, :].ap), rhs=UD[:, sg_].ap,
                                            start=(sg_ == 0), stop=(NOCARRY and sg_ == tau))
                        for q4 in (range(0) if NOCARRY else range(4)):
                            for ri in range(2):
                                ins = pe.matmul(yo.ap[32 * q4:32 * q4 + 32, 1:256], lhsT=WC[:, q4, ri, tau + 1, :].ap,
                                                rhs=HPREV[:, q4, ri, 0:255].ap, start=False,
                                                stop=(ri == 1), tile_position=(0, 32 * q4))
                        return ins
                    if MMTEST:
                        mm(yo, [(KMAT[:, tau - sg_, :], UD[:, sg_]) for sg_ in range(tau + 1)])
                        for q4 in range(4):
                            yq = yo.w(yo.ap[32 * q4:32 * q4 + 32, 1:256])
                            mm(yq, [(WC[:, q4, ri, tau + 1, :], HPREV[:, q4, ri, 0:255]) for ri in range(2)],
                               tile_position=(0, 32 * q4), start=False)
                    else:
                        P.add("pe", fn, [yo], rd)
                    chk(61 + tau)
                chk(6)
                for b in range(4):
                    uv = UA[:, pt]
                    uv = uv.w(uv.ap.rearrange("p (c s) -> p s c", s=8)[:, 2 * b:2 * b + 2, :])
                    bv = ybs[b][:]
                    bv = bv.w(bv.ap.rearrange("p (s c) -> p s c", s=2))
                    ys = YS[b % 2][:]
                    stt(ys, UD[:, 2 * b:2 * b + 2], col(c_dsk + pt), bv, ALU.mult, ALU.add)
                    act(uv, ys, AF.Gelu_apprx_tanh)
                chk(7)

            if l == 0:
                dump("g", UA[:], [4, 2048], BF16)
            (WGL,) = wload([(d_wglu[l].rearrange("(kt p) n -> p kt n", p=128), [4, 512])])
            SG = mk(o_MISC, [2, 512], F32)
            for tb in range(NTB):
                bks = []
                for m in range(4):
                    bk = nb()
                    mm(bk[:], [(WGL[:, kt, m * 128:(m + 1) * 128], UA[:, kt, tb * 512:(tb + 1) * 512]) for kt in range(4)])
                    bks.append(bk)
                for m in range(4):
                    sg = SG[:, m % 2]
                    act(sg, bks[m][:], AF.Sigmoid, bias=col(c_bglu + m), scale=1.0)
                    tt("dve", UA[:, m, tb * 512:(tb + 1) * 512], UA[:, m, tb * 512:(tb + 1) * 512], sg, ALU.mult)

            if l == 0:
                dump("s5", UA[:], [4, 2048], BF16)
            for k in range(4):
                memset("dve", UC[:, k, 0:16], 0.0)
                memset("dve", HC[:, k, 0:30], 0.0)
            for c in range(2):
                (W,) = wload([(win[:, :, 1536 + c * 256:1536 + (c + 1) * 256], [KT, 256])])
                for tb in range(NTB):
                    for mm_ in range(2):
                        m = 2 * c + mm_
                        bk = nb()
                        mm(bk[:], [(W[:, kt, mm_ * 128:(mm_ + 1) * 128], H[:, tb, kt]) for kt in range(KT)])
                        act(UC[:, m, 16 + tb * 512:16 + tb * 512 + 512], bk[:], AF.Copy)
            (WGR,) = wload([(d_wgrp[l].rearrange("g c d -> c g d"), [4, 128])])
            PW = mk(o_MISC, [2, 528], F32)
            PP = mk(o_MISC + 4224, [2, 512], BF16)
            TW = mk(o_MISC + 6272, [16], F32)
            pst = {"it": 0}

            def pool_dve(k, tb):
                w = 2 ** (k + 1)
                b0 = tb * 512
                U = lambda a, b_: UC[:, k, b0 + a:b0 + b_]
                tt("dve", PW[:, 0, 1:528], U(1, 528), U(0, 527), ALU.add)
                fin = 0
                if k >= 1:
                    tt("dve", PW[:, 1, 3:528], PW[:, 0, 3:528], PW[:, 0, 1:526], ALU.add)
                    fin = 1
                if k >= 2:
                    tt("dve", PW[:, 0, 7:528], PW[:, 1, 7:528], PW[:, 1, 3:524], ALU.add)
                    fin = 0
                if k >= 3:
                    tt("dve", PW[:, 1, 15:528], PW[:, 0, 15:528], PW[:, 0, 7:520], ALU.add)
                    fin = 1
                pp = PP[:, pst["it"] % 2]
                pst["it"] += 1
                stt(pp, PW[:, fin, 16:528], 1.0 / w, U(16, 528), ALU.mult, ALU.subtract)
                if tb == 0:
                    tt("dve", TW[:], PW[:, fin, 16:32], INVT[:, k], ALU.mult)
                    tt("dve", pp.w(pp.ap[:, 0:16]), TW[:], U(16, 32), ALU.subtract)
                return pp

            def pool_pe(k, tb, pp):
                b0 = tb * 512
                bk = nb()
                mm(bk[:], [(WGR[:, k], pp)])
                ts("dve", UC[:, k, b0 + 16:b0 + 528], bk[:], col(c_psc + k), ALU.mult)

            items = [(k, tb) for k in range(4) for tb in (3, 2, 1, 0)]
            SGc = mk(o_MISC + 8192, [2, 512], F32)
            gi = 0
            for c in range(2):
                (WA,) = wload([(win[:, :, 512 + c * 256:512 + (c + 1) * 256], [KT, 256])])
                (WB,) = wload([(win[:, :, 1024 + c * 256:1024 + (c + 1) * 256], [KT, 256])])
                for tb in range(NTB):
                    for mm_ in range(2):
                        m = 2 * c + mm_
                        pk_, ptb_ = items[gi]
                        pp = pool_dve(pk_, ptb_)
                        ba, bb_ = nb(), nb()
                        mm(ba[:], [(WA[:, kt, mm_ * 128:(mm_ + 1) * 128], H[:, tb, kt]) for kt in range(KT)])
                        mm(bb_[:], [(WB[:, kt, mm_ * 128:(mm_ + 1) * 128], H[:, tb, kt]) for kt in range(KT)])
                        pool_pe(pk_, ptb_, pp)
                        sg = SGc[:, gi % 2]
                        act(sg, bb_[:], AF.Sigmoid)
                        tt("dve", HC[:, m, 30 + tb * 512:30 + tb * 512 + 512], ba[:], sg, ALU.mult)
                        gi += 1

            if l == 0:
                dump("poolout", UC[:], [4, 2064], BF16)
            if l == 0:
                dump("hc", HC[:], [4, 2078], BF16)
            CSQ = mk(o_MISC, [4, 512], BF16)
            MU = mk(o_MISC + 4096, [2, 512], F32)
            RSTD = mk(o_MISC + 8192, [2, 512], F32)
            M2 = mk(o_MISC + 12288, [512], F32)
            T1 = mk(o_MISC + 14336, [512], F32)
            for hf in (1, 0):
                for m in range(4):
                    (DA,) = wload([(d_convd[l, m, :, 0:16, :], [16, 128])])
                    (DB,) = wload([(d_convd[l, m, :, 16:31, :], [15, 128])])
                    for tbl in range(2):
                        tb = 2 * hf + tbl
                        bk = nb()
                        prs = [((DA[:, k] if k < 16 else DB[:, k - 16]), HC[:, m, tb * 512 + k:tb * 512 + k + 512]) for k in range(31)]
                        mm(bk[:], prs)
                        act(CV[:, tbl, m], bk[:], AF.Identity, bias=col(c_cb + m), scale=1.0)
                for tbl in range(2):
                    act(CSQ[:], CV[:, tbl], AF.Square)
                    bmu, bsq = nb(), nb()
                    mm(bmu[:], [(ONESF[:], CV[:, tbl, m]) for m in range(4)])
                    mm(bsq[:], [(ONESB[:], CSQ[:, m]) for m in range(4)])
                    act(MU[:, tbl], bmu[:], AF.Identity)
                    tt("dve", M2[:], MU[:, tbl], MU[:, tbl], ALU.mult)
                    stt(M2[:], bsq[:], 2.0, M2[:], ALU.mult, ALU.subtract)
                    act(RSTD[:, tbl], M2[:], AF.Sqrt, bias=EPSC[:], scale=1.0)
                    P.add("dve", (lambda e, tbl=tbl: e.reciprocal(out=RSTD[:, tbl].ap, in_=RSTD[:, tbl].ap)), [RSTD[:, tbl]], [RSTD[:, tbl]])
                for tbl in range(2):
                    tb = 2 * hf + tbl
                    for m in range(4):
                        tq = (T1, M2)[m % 2]
                        tt("dve", tq[:], CV[:, tbl, m], MU[:, tbl], ALU.subtract)
                        tt("dve", tq[:], tq[:], RSTD[:, tbl], ALU.mult)
                        act(HC[:, m, 30 + tb * 512:30 + tb * 512 + 512], tq[:], AF.Silu, bias=col(c_lnb + m), scale=col(c_lng + m))

            if l == 0:
                dump("convout", HC[:], [4, 2078], BF16)
            SGm = mk(o_MISC, [2, 512], F32)
            ACC = mk(o_MISC + 4096, [2, 512], F32)
            TMP = mk(o_MISC + 8192, [2, 512], F32)
            projs = (d_wpa, d_wpb, d_wpc)

            def branch(k, tb, kt):
                if k == 0:
                    return UA[:, kt, tb * 512:(tb + 1) * 512]
                if k == 1:
                    return HC[:, kt, 30 + tb * 512:30 + tb * 512 + 512]
                return UC[:, kt, 16 + tb * 512:16 + tb * 512 + 512]

            it = 0
            for hf in (1, 0):
                for j in range(8):
                    ws = []
                    for k in range(3):
                        gsrc = win[:, :, 2048 + k * 1024 + j * 128:2048 + k * 1024 + (j + 1) * 128]
                        psrc = projs[k][l].rearrange("(kt p) n -> p kt n", p=128)[:, :, j * 128:(j + 1) * 128]
                        ws.append(wload([(gsrc, [KT, 128]), (psrc, [4, 128])]))
                    for tbl in range(2):
                        tb = 2 * hf + tbl
                        par = it % 2
                        it += 1
                        for k in range(3):
                            WG_, WP_ = ws[k]
                            bg, by = nb(), nb()
                            mm(bg[:], [(WG_[:, kt], H[:, tb, kt]) for kt in range(KT)])
                            mm(by[:], [(WP_[:, kt], branch(k, tb, kt)) for kt in range(4)])
                            sg = SGm[:, k % 2]
                            act(sg, bg[:], AF.Sigmoid, bias=col(c_bg + k * 8 + j), scale=1.0)
                            if k == 0:
                                tt("dve", ACC[:, par], by[:], sg, ALU.mult)
                            else:
                                tt("dve", TMP[:, k - 1], by[:], sg, ALU.mult)
                        tt("dve", ACC[:, par], ACC[:, par], TMP[:, 0], ALU.add)
                        tt("dve", MERGED[:, tbl, j], ACC[:, par], TMP[:, 1], ALU.add)
                wo = d_wout[l].rearrange("(kt p) n -> p kt n", p=128)
                for dc in range(4):
                    (WO,) = wload([(wo[:, :, dc * 256:(dc + 1) * 256], [KT, 256])])
                    for tbl in range(2):
                        tb = 2 * hf + tbl
                        for dd in range(2):
                            dp = 2 * dc + dd
                            bk = nb()
                            mm(bk[:], [(WO[:, j, dd * 128:(dd + 1) * 128], MERGED[:, tbl, j]) for j in range(8)])
                            tt("dve", X[:, tb, dp], bk[:], X[:, tb, dp], ALU.add)

            if l == 0:
                dump("xmix", X[:], [NTB, KT, 512], F32)
            for tb in range(NTB):
                rmsnorm(tb, c_g2, H)
            wg = d_wg[l].rearrange("(kt p) n -> p kt n", p=128)
            wu = d_wu[l].rearrange("(kt p) n -> p kt n", p=128)
            SGf = mk(o_MISC, [2, 512], F32)
            it = 0
            pend = capture(lambda: ssm_layer_gen(l + 1)) if (HOIST and l + 1 < NL) else []
            for (f0, nf) in ((0, 8), (8, 8), (16, 6)):
                for fi in range(nf):
                    f = f0 + fi
                    WG_, WU_ = wload([(wg[:, :, f * 128:(f + 1) * 128], [KT, 128]), (wu[:, :, f * 128:(f + 1) * 128], [KT, 128])])
                    for tb in range(NTB):
                        bg, bu = nb(), nb()
                        mm(bg[:], [(WG_[:, kt], H[:, tb, kt]) for kt in range(KT)])
                        mm(bu[:], [(WU_[:, kt], H[:, tb, kt]) for kt in range(KT)])
                        sg = SGf[:, it % 2]
                        it += 1
                        act(sg, bg[:], AF.Silu)
                        tt("dve", ACTB[:, tb, fi], bu[:], sg, ALU.mult)
                        if f0 >= 8:
                            for _ in range(3):
                                if pend:
                                    P.ops.append(pend.pop(0))
                wds = []
                for s in range(nf // 2):
                    (WD,) = wload([(d_wd[l, (f0 + 2 * s) * 128:(f0 + 2 * s + 2) * 128, :].rearrange("(ft p) n -> p ft n", p=128), [2, 1024])])
                    wds.append(WD)
                def down(dp, tb, nf=nf, wds=wds):
                    bk = nb()
                    mm(bk[:], [(wds[fi // 2][:, fi % 2, dp * 128:(dp + 1) * 128], ACTB[:, tb, fi]) for fi in range(nf)])
                    tt("dve", X[:, tb, dp], bk[:], X[:, tb, dp], ALU.add)

                if TAIL_NORM and f0 == 16 and not STOP:
                    def next_norm(tb):
                        if l == 0 and DEBUG:
                            dump("xl0_%d" % tb, X[:, tb], [KT, 512], F32)
                        if l + 1 < NL:
                            rmsnorm(tb, (l + 1) * LC, H)
                        else:
                            rmsnorm(tb, NL * LC, X)
                            P.add("sp", (lambda e, tb=tb: e.dma_start(out=d_out[:, tb], in_=X[:, tb].ap)), [], [X[:, tb]], dma=True, final=True)
                    for tb in range(NTB):
                        for dp in range(8):
                            down(dp, tb)
                        if tb >= 1:
                            next_norm(tb - 1)
                    next_norm(NTB - 1)
                else:
                    for dp in range(8):
                        for tb in range(NTB):
                            down(dp, tb)

        try:
            if HOIST:
                ssm_layer_gen(0)
            for l in range(NL):
                layer(l)
                if l == 0 and not TAIL_NORM:
                    dump("xl0", X[:], [NTB, KT, 512], F32)
        except _Stop:
            pass
        if not TAIL_NORM or STOP:
            for tb in range(NTB):
                rmsnorm(tb, NL * LC, X)
                P.add("sp", (lambda e, tb=tb: e.dma_start(out=d_out[:, tb], in_=X[:, tb].ap)), [], [X[:, tb]], dma=True, final=True)

    ctx["body"] = body
    ctx["P"] = P
    ctx["TOTAL"] = TOTAL
    return nc, ctx


def emit(nc, ctx):
    P = ctx["P"]
    ENG = ["pe", "act", "dve", "pool", "sp"]
    with (
        nc.sbuf_tensor("raw", [128, ctx["TOTAL"] // 4], F32) as RAW,
        nc.psum_tensor("ps", [128, 4096], F32) as PS,
    ):
        ctx["body"](RAW, PS)
        ops = P.ops
        n = len(ops)
        trk = {"S": Track(), "P": Track()}
        deps = []
        for i, (eng, fn, outs, ins, dma, final) in enumerate(ops):
            d = set()
            for v in ins:
                d |= trk[v.sp].access(v.lo, v.hi, i, eng, dma, False)
            for v in outs:
                d |= trk[v.sp].access(v.lo, v.hi, i, eng, dma, True)
            d.discard(i)
            deps.append(d)
        signal = [False] * n
        for i in range(n):
            ei, di = ops[i][0], ops[i][4]
            for j in deps[i]:
                ej, dj = ops[j][0], ops[j][4]
                if dj:
                    continue
                if ej == ei and not di and (ei == "pe" or not SAME_SYNC):
                    continue
                signal[j] = True
        cnt = {e: 0 for e in ENG}
        sigval = [0] * n
        dq = {"sp": 0, "pool": 0}
        uses = {(q, s): 0 for q in dq for s in range(NDMASEM)}
        dslot = [None] * n
        dval = [0] * n
        for i in range(n):
            eng, _, _, _, dma, _ = ops[i]
            if dma:
                s = dq[eng] % NDMASEM
                dq[eng] += 1
                uses[(eng, s)] += 1
                dslot[i] = (eng, s)
                dval[i] = 16 * uses[(eng, s)]
            elif signal[i]:
                cnt[eng] += 1
                sigval[i] = cnt[eng]
        import contextlib
        with contextlib.ExitStack() as es:
            esem = {e: es.enter_context(nc.semaphore("s_" + e)) for e in ENG}
            dsem = {(q, s): es.enter_context(nc.semaphore(f"d_{q}{s}")) for q in dq for s in range(NDMASEM)}
            block = es.enter_context(nc.Block())

            def stream(ename, handle):
                waited = {}

                def wait(sem, key, val):
                    if val <= 0 or waited.get(key, 0) >= val:
                        return
                    waited[key] = val
                    handle.wait_ge(sem, val)

                finals = []
                for i in range(n):
                    eng, fn, outs, ins, dma, final = ops[i]
                    if eng != ename:
                        continue
                    for j in sorted(deps[i]):
                        ej, dj = ops[j][0], ops[j][4]
                        if dj:
                            wait(dsem[dslot[j]], ("d",) + dslot[j], dval[j])
                        elif ej == ename and not dma and (ename == "pe" or not SAME_SYNC):
                            continue
                        else:
                            wait(esem[ej], ("e", ej), sigval[j])
                    if dma:
                        wait(dsem[dslot[i]], ("d",) + dslot[i], dval[i] - 16)
                    ins_ = fn(handle)
                    if dma:
                        ins_.then_inc(dsem[dslot[i]], 16)
                        if final:
                            finals.append(i)
                    elif signal[i]:
                        ins_.then_inc(esem[ename], 1)
                for i in finals:
                    wait(dsem[dslot[i]], ("d",) + dslot[i], dval[i])

            @block.sync
            def _(h):
                stream("sp", h)

            @block.gpsimd
            def _(h):
                stream("pool", h)

            @block.scalar
            def _(h):
                stream("act", h)

            @block.vector
            def _(h):
                stream("dve", h)

            @block.tensor
            def _(h):
                stream("pe", h)
    return nc


def _prep_shared(inp):
    f = lambda a: np.ascontiguousarray(np.asarray(a, dtype=np.float32))
    cols = np.zeros((128, NCOL), np.float32)

    def tiles(v, nt):
        return np.asarray(v, np.float32).reshape(nt, 128).T

    for l in range(NL):
        cb = l * LC
        cols[:, cb:cb + 8] = tiles(inp["norm1"][l], 8)
        cols[:, cb + 8:cb + 16] = tiles(inp["norm2"][l], 8)
        cols[:, cb + 16:cb + 40] = tiles(inp["b_gate"][l], 24)
        cols[:, cb + 40:cb + 44] = tiles(inp["ssm_b_glu"][l], 4)
        cols[:, cb + 44:cb + 48] = tiles(inp["conv_b_dw"][l], 4)
        cols[:, cb + 48:cb + 52] = tiles(inp["conv_ln_g"][l], 4)
        cols[:, cb + 52:cb + 56] = tiles(inp["conv_ln_b"][l], 4)
        cols[:, cb + 56:cb + 60] = tiles(inp["pool_scale"][l], 4)
        cols[:, cb + 60:cb + 64] = tiles(np.asarray(inp["ssm_d"][l]).reshape(512), 4)
        are = np.asarray(inp["ssm_a_re"][l], np.float32)
        aim = np.asarray(inp["ssm_a_im"][l], np.float32)
        ldt = np.asarray(inp["ssm_log_dt"][l], np.float32)
        cols[:, cb + 64:cb + 80] = are.reshape(16, 2, 64).transpose(1, 2, 0).reshape(128, 16)
        cols[:, cb + 80:cb + 96] = aim.reshape(16, 2, 64).transpose(1, 2, 0).reshape(128, 16)
        cols[:, cb + 96:cb + 112] = np.broadcast_to(ldt.reshape(16, 2, 1), (16, 2, 64)).transpose(1, 2, 0).reshape(128, 16)
    cols[:, NL * LC:NL * LC + 8] = tiles(inp["final_norm"], 8)

    ssmw = np.zeros((NL, 128, 5, 4, 128), np.float32)
    ssmc = np.zeros((NL, 128, 16, 2, 32), np.float32)
    ssmb = np.zeros((NL, 128, 16, 2, 32), np.float32)
    for l in range(NL):
        are = np.asarray(inp["ssm_a_re"][l], np.float32)
        aim = np.asarray(inp["ssm_a_im"][l], np.float32)
        ldt = np.asarray(inp["ssm_log_dt"][l], np.float32)
        bre = np.asarray(inp["ssm_b_re"][l], np.float32)
        bim = np.asarray(inp["ssm_b_im"][l], np.float32)
        cre = np.asarray(inp["ssm_c_re"][l], np.float32)
        cim = np.asarray(inp["ssm_c_im"][l], np.float32)
        for pt in range(4):
            for q4 in range(4):
                q = 4 * pt + q4
                rows = slice(32 * q4, 32 * q4 + 32)
                for g2p in range(2):
                    g = 2 * q + g2p
                    cs = slice(64 * g2p, 64 * g2p + 64)
                    ssmw[l, rows, 0, pt, cs] = are[g][None, :]
                    ssmw[l, rows, 1, pt, cs] = aim[g][None, :]
                    ssmw[l, rows, 2, pt, cs] = ldt[g]
                    r2 = slice(32 * q4 + 16 * g2p, 32 * q4 + 16 * g2p + 16)
                    ssmw[l, r2, 3, pt, cs] = bre[g].T
                    ssmw[l, r2, 4, pt, cs] = bim[g].T
        for q in range(16):
            for g2 in range(2):
                g = 2 * q + g2
                ssmc[l, 64 * g2:64 * g2 + 64, q, 0, 16 * g2:16 * g2 + 16] = cre[g].T
                ssmc[l, 64 * g2:64 * g2 + 64, q, 1, 16 * g2:16 * g2 + 16] = cim[g].T
                ssmb[l, 64 * g2:64 * g2 + 64, q, 0, 16 * g2:16 * g2 + 16] = bre[g]
                ssmb[l, 64 * g2:64 * g2 + 64, q, 1, 16 * g2:16 * g2 + 16] = bim[g]
    cw = np.asarray(inp["conv_w_dw"], np.float32)[:, :, 0, :]
    convd = np.zeros((NL, 4, 128, 31, 128), np.float32)
    ar = np.arange(128)
    for l in range(NL):
        for m in range(4):
            convd[l, m, ar, :, ar] = cw[l, :, m * 128:(m + 1) * 128].T
    sh = {
        "cols": cols, "ssmw": ssmw, "ssmc": ssmc, "ssmb": ssmb, "convd": convd,
        "w_in": f(inp["w_in"]), "w_glu": f(inp["ssm_w_glu"]), "w_pa": f(inp["ssm_w_proj"]),
        "w_pb": f(inp["conv_w_proj"]), "w_pc": f(inp["pool_w_proj"]), "w_grp": f(inp["pool_w_group"]),
        "w_out": f(inp["w_out"]), "w_g": f(inp["ffn_w_gate"]), "w_u": f(inp["ffn_w_up"]), "w_d": f(inp["ffn_w_down"]),
    }
    return sh


def kernel(**inputs):
    x = np.asarray(inputs["x"], np.float32)
    sh = _prep_shared(inputs)
    nc, ctx = build_program()
    emit(nc, ctx)
    in_maps = []
    for c in range(8):
        xt = x[c].reshape(NTB, 512, KT, 128).transpose(3, 0, 2, 1)
        m = dict(sh)
        m["xT"] = np.ascontiguousarray(xt)
        in_maps.append(m)
    res = run_bass_kernel_spmd(nc, in_maps, core_ids=list(range(8)))
    out = np.empty((8, T, 1024), np.float32)
    for c in range(8):
        o = np.asarray(res.results[c]["outT"], np.float32)
        out[c] = o.transpose(1, 3, 2, 0).reshape(T, 1024)
    return out
```

```python
import bisect
import numpy as np
import concourse.bass as bass
import concourse.mybir as mybir
from concourse.bass_utils import run_bass_kernel_spmd

F32 = mybir.dt.float32
BF16 = mybir.dt.bfloat16
AF = mybir.ActivationFunctionType
ALU = mybir.AluOpType
MAGIC = 12582912.0
TWO_PI = 6.283185307179586

NL = 2
T = 2048
NTB = 4
KT = 8
NF = 22
LC = 112
NCOL = NL * LC + 8
NSLOT_W = 6
SLOTB = 4096
NDMASEM = 8
HOIST = True
TAIL_NORM = True
DEBUG = False
SAME_SYNC = True
DBG = {}
STOP = 0
MMTEST = False
KTEST = False
NOCARRY = False


class _Stop(Exception):
    pass


def chk(n):
    if STOP == n:
        raise _Stop()


class V:
    __slots__ = ("ap", "sp", "lo", "hi")

    def __init__(self, ap, sp, lo, hi):
        self.ap, self.sp, self.lo, self.hi = ap, sp, lo, hi

    def w(self, ap):
        return V(ap, self.sp, self.lo, self.hi)


class Tile:
    def __init__(self, ap, sp, off, dims, esz):
        self.ap, self.sp, self.off, self.dims, self.esz = ap, sp, off, list(dims), esz
        st, acc = [], 1
        for d in reversed(self.dims):
            st.append(acc)
            acc *= d
        self.st = list(reversed(st))

    def __getitem__(self, idx):
        if not isinstance(idx, tuple):
            idx = (idx,)
        free = idx[1:]
        lo = hi = 0
        for i, d in enumerate(self.dims):
            if i < len(free):
                ix = free[i]
                if isinstance(ix, int):
                    a = b = ix
                else:
                    stp = ix.step or 1
                    a = ix.start or 0
                    stop = d if ix.stop is None else ix.stop
                    cnt = (stop - a + stp - 1) // stp
                    b = a + (cnt - 1) * stp
            else:
                a, b = 0, d - 1
            lo += a * self.st[i]
            hi += b * self.st[i]
        if self.sp == "P":
            return V(self.ap[idx], self.sp, self.off, self.off + 2048)
        return V(self.ap[idx], self.sp, self.off + lo * self.esz, self.off + (hi + 1) * self.esz)


class Track:
    def __init__(self):
        self.los = []
        self.recs = []

    def access(self, lo, hi, op, eng, is_dma, is_write):
        deps = set()
        i = bisect.bisect_right(self.los, lo) - 1
        if i < 0:
            i = 0
        new = []
        j = i
        cur = lo
        while j < len(self.recs) and self.recs[j][0] < hi:
            r = self.recs[j]
            if r[1] <= lo:
                j += 1
                i = j
                continue
            if r[2] is not None:
                deps.add(r[2])
            if is_write:
                deps.update(r[3].values())
                deps.update(r[4])
            if r[0] < lo:
                new.append([r[0], lo, r[2], dict(r[3]), list(r[4])])
            a, b = max(r[0], lo), min(r[1], hi)
            if not is_write:
                if cur < a:
                    new.append([cur, a, None, ({} if is_dma else {eng: op}), ([op] if is_dma else [])])
                rd = dict(r[3])
                dl = list(r[4])
                if is_dma:
                    dl.append(op)
                else:
                    rd[eng] = op
                new.append([a, b, r[2], rd, dl])
            cur = b
            if r[1] > hi:
                new.append([hi, r[1], r[2], dict(r[3]), list(r[4])])
            j += 1
        if is_write:
            lead = [x for x in new if x[1] <= lo]
            trail = [x for x in new if x[0] >= hi]
            new = lead + [[lo, hi, op, {}, []]] + trail
        else:
            if cur < hi:
                new.append([cur, hi, None, ({} if is_dma else {eng: op}), ([op] if is_dma else [])])
        self.recs[i:j] = new
        self.los[i:j] = [x[0] for x in new]
        return deps


class Prog:
    def __init__(self):
        self.ops = []

    def add(self, eng, fn, outs=(), ins=(), dma=False, final=False):
        self.ops.append((eng, fn, list(outs), list(ins), dma, final))


def build_program():
    nc = bass.Bass("TRN2", target_bir_lowering=False)

    def din(name, shape):
        return nc.dram_tensor(name, list(shape), F32, kind="ExternalInput").ap()

    d_x = din("xT", [128, NTB, KT, 512])
    d_cols = din("cols", [128, NCOL])
    d_ssmw = din("ssmw", [NL, 128, 5, 4, 128])
    d_ssmc = din("ssmc", [NL, 128, 16, 2, 32])
    d_ssmb = din("ssmb", [NL, 128, 16, 2, 32])
    d_win = din("w_in", [NL, 1024, 5120])
    d_wglu = din("w_glu", [NL, 512, 512])
    d_wpa = din("w_pa", [NL, 512, 1024])
    d_wpb = din("w_pb", [NL, 512, 1024])
    d_wpc = din("w_pc", [NL, 512, 1024])
    d_convd = din("convd", [NL, 4, 128, 31, 128])
    d_wgrp = din("w_grp", [NL, 4, 128, 128])
    d_wout = din("w_out", [NL, 1024, 1024])
    d_wg = din("w_g", [NL, 1024, 2816])
    d_wu = din("w_u", [NL, 1024, 2816])
    d_wd = din("w_d", [NL, 2816, 1024])
    d_out = nc.dram_tensor("outT", [128, NTB, KT, 512], F32, kind="ExternalOutput").ap()

    o_X = 0
    o_H = o_X + 65536
    o_UA = o_H + 32768
    o_UC = o_UA + 16384
    o_HC = o_UC + 4 * 2064 * 2
    o_CVM = o_HC + 4 * 2078 * 2
    o_SL = o_CVM + 16384
    o_MISC = o_SL + NSLOT_W * SLOTB
    o_CONST = o_MISC + 16384
    TOTAL = o_CONST + 4932
    assert o_CVM % 4 == 0

    P = Prog()
    ctx = {}

    def body(RAW, PS):
        def mk(off, dims, dt):
            esz = 4 if dt == F32 else 2
            n = int(np.prod(dims))
            nb = n * esz
            assert off % 4 == 0 and nb % 4 == 0, (off, dims)
            a = RAW[:, off // 4:(off + nb) // 4]
            if dt != F32:
                a = a.bitcast(dt)
            if len(dims) == 2:
                a = a.rearrange("p (a b) -> p a b", a=dims[0])
            elif len(dims) == 3:
                a = a.rearrange("p (a b c) -> p a b c", a=dims[0], b=dims[1])
            elif len(dims) == 4:
                a = a.rearrange("p (a b c d) -> p a b c d", a=dims[0], b=dims[1], c=dims[2])
            return Tile(a, "S", off, dims, esz)

        X = mk(o_X, [NTB, KT, 512], F32)
        H = mk(o_H, [NTB, KT, 512], BF16)
        UA = mk(o_UA, [4, 2048], BF16)
        UC = mk(o_UC, [4, 2064], BF16)
        HC = mk(o_HC, [4, 2078], BF16)
        CV = mk(o_CVM, [2, 4, 512], F32)
        MERGED = mk(o_CVM, [2, 8, 512], BF16)
        ACTB = mk(o_UA, [NTB, 8, 512], BF16)
        COLS = mk(o_CONST, [NCOL], F32)
        ONESB = mk(o_CONST + 928, [128], BF16)
        ONESF = mk(o_CONST + 1184, [128], F32)
        INVT = mk(o_CONST + 1696, [4, 16], F32)
        IDX = mk(o_CONST + 1952, [512], F32)
        EPSC = mk(o_CONST + 4000, [1], F32)
        SSTATE = mk(o_CONST + 4004, [14, 16], F32)
        K7R = mk(o_CONST + 4900, [8], F32)
        banks = [Tile(PS[:, b * 512:(b + 1) * 512], "P", b * 2048, [512], 4) for b in range(8)]
        st = {"bank": 0, "slot": 0}

        def nb():
            b = banks[st["bank"] % 8]
            st["bank"] += 1
            return b

        def col(i):
            return COLS[:, i:i + 1]

        def tt(eng, out, a, b, op):
            P.add(eng, lambda e: e.tensor_tensor(out=out.ap, in0=a.ap, in1=b.ap, op=op), [out], [a, b])

        def ts(eng, out, a, s1, op0, s2=None, op1=None):
            ins = [a] + [s for s in (s1, s2) if isinstance(s, V)]
            s1a = s1.ap if isinstance(s1, V) else s1
            s2a = s2.ap if isinstance(s2, V) else s2
            if op1 is None:
                P.add(eng, lambda e: e.tensor_scalar(out=out.ap, in0=a.ap, scalar1=s1a, scalar2=None, op0=op0), [out], ins)
            else:
                P.add(eng, lambda e: e.tensor_scalar(out=out.ap, in0=a.ap, scalar1=s1a, scalar2=s2a, op0=op0, op1=op1), [out], ins)

        def stt(out, a, s, b, op0, op1):
            ins = [a, b] + ([s] if isinstance(s, V) else [])
            sa = s.ap if isinstance(s, V) else s
            P.add("dve", lambda e: e.scalar_tensor_tensor(out=out.ap, in0=a.ap, scalar=sa, in1=b.ap, op0=op0, op1=op1), [out], ins)

        def act(out, a, func, bias=None, scale=None):
            ins = [a] + [s for s in (bias, scale) if isinstance(s, V)]
            kw = {}
            if bias is not None:
                kw["bias"] = bias.ap if isinstance(bias, V) else bias
            if scale is not None:
                kw["scale"] = scale.ap if isinstance(scale, V) else scale
            P.add("act", lambda e: e.activation(out=out.ap, in_=a.ap, func=func, **kw), [out], ins)

        def memset(eng, out, val):
            P.add(eng, lambda e: e.memset(out.ap, val), [out], [])

        def mm(out, pairs, tile_position=None, start=True):
            def fn(pe):
                n = len(pairs)
                ins = None
                for i, (l, r) in enumerate(pairs):
                    kw = {}
                    if tile_position is not None:
                        kw["tile_position"] = tile_position
                    ins = pe.matmul(out.ap, lhsT=l.ap, rhs=r.ap, start=(start and i == 0), stop=(i == n - 1), **kw)
                return ins
            P.add("pe", fn, [out], [v for p in pairs for v in p])

        def wload(parts):
            s = st["slot"] % NSLOT_W
            st["slot"] += 1
            off = o_SL + s * SLOTB
            tiles = []
            for src, dims in parts:
                t_ = mk(off, dims, BF16)
                off += int(np.prod(dims)) * 2
                assert off <= o_SL + (s + 1) * SLOTB
                P.add("pool", (lambda e, t_=t_, src=src: e.dma_start(out=t_.ap, in_=src)), [t_[:]], [], dma=True)
                tiles.append(t_)
            return tiles

        def sload(tile, src):
            P.add("sp", lambda e: e.dma_start(out=tile.ap, in_=src), [tile], [], dma=True)

        def dump(name, v, shape, dt):
            if not DEBUG:
                return
            d = nc.dram_tensor("dbg_" + name, [128] + list(shape), dt, kind="ExternalOutput").ap()
            DBG[name] = (shape, dt)
            P.add("sp", lambda e: e.dma_start(out=d, in_=v.ap), [], [v], dma=True, final=True)

        ctx["dump"] = dump

        def wrap(eng, x, tmp):
            ts(eng, tmp, x, MAGIC, ALU.add, MAGIC, ALU.subtract)
            tt(eng, x, x, tmp, ALU.subtract)

        sload(COLS[:], d_cols[:, :])
        memset("dve", ONESB[:], 1.0 / 1024.0)
        memset("dve", ONESF[:], 1.0 / 512.0)
        memset("dve", EPSC[:], 1e-6)
        for k in range(4):
            w = 2 ** (k + 1)
            memset("dve", INVT[:, k, :], 1.0 / w)
            for t_ in range(w - 1):
                memset("dve", INVT[:, k, t_:t_ + 1], 1.0 / (t_ + 1))
        memset("dve", IDX[:, 0:1], 0.0)
        n = 1
        while n < 512:
            ts("dve", IDX[:, n:2 * n], IDX[:, 0:n], float(n), ALU.add)
            n *= 2
        ts("dve", K7R[:], IDX[:, 0:8], -1.0, ALU.mult, 7.0, ALU.add)
        for tb in range(NTB):
            sload(X[:, tb], d_x[:, tb])
        ctx["pre_gen"] = True

        SQ = mk(o_MISC, [KT, 512], BF16)
        RS = mk(o_MISC + 8192, [2, 512], F32)

        def rmsnorm(tb, gbase, dst, inplace=False):
            act(SQ[:], X[:, tb], AF.Square)
            bk = nb()
            mm(bk[:], [(ONESB[:], SQ[:, kt]) for kt in range(KT)])
            rs = RS[:, tb % 2]
            act(rs, bk[:], AF.Sqrt, bias=EPSC[:], scale=1.0)
            P.add("dve", lambda e: e.reciprocal(out=rs.ap, in_=rs.ap), [rs], [rs])
            for kt in range(KT):
                stt(dst[:, tb, kt], X[:, tb, kt], col(gbase + kt), rs, ALU.mult, ALU.mult)

        so = o_UC
        hb = o_HC
        W1PAD = mk(so, [4, 8, 2, 128], BF16)
        PRMC = mk(hb, [16, 2, 32], F32)
        FBW = mk(hb + 4096, [2, 4, 128], F32)
        DFW = mk(hb + 8192, [2, 4, 128], F32)
        FBT = mk(hb + 12288, [16, 2, 32], BF16)
        PK = mk(hb + 14336, [2, 16, 9], F32)
        KMAT = mk(hb + 15616, [8, 128], BF16)
        T1o = hb + 17664
        PRMW = mk(T1o, [5, 4, 128], F32)
        PRMB = mk(hb + 27904, [16, 2, 32], F32)
        assert hb + 32000 <= o_SL
        G = [mk(o_MISC + 4096 + i * 2048, [4, 128], F32) for i in range(6)]
        WC = mk(o_MISC, [4, 2, 9, 32], BF16)
        YS = [mk(o_MISC + 4608, [2, 256], F32), mk(hb + 27904, [2, 256], F32)]
        HPREV = mk(o_MISC + 6656, [4, 2, 256], BF16)
        UD = mk(o_MISC + 10752, [8, 256], BF16)

        def bc(v, axis, shape):
            return v.w(v.ap.unsqueeze(axis).to_broadcast(shape))

        def ps_(v, a, b):
            return v.w(v.ap[a:b])

        def ssm_layer_gen(l):
            c_ss = l * LC + 64
            sload(PRMW[:], d_ssmw[l])
            sload(PRMC[:], d_ssmc[l])
            sload(PRMB[:], d_ssmb[l])
            aRe, aIm, lgd, bRe, bIm = (PRMW[:, i] for i in range(5))
            dtw, mag, sn, cs, t0, t1 = (G[i][:] for i in range(6))
            drew, frw = DFW[:, 0], DFW[:, 1]
            act(dtw, lgd, AF.Exp)
            tt("dve", drew, dtw, aRe, ALU.mult)
            stt(frw, aIm, 1.0 / TWO_PI, dtw, ALU.mult, ALU.mult)
            wrap("dve", frw, t0)
            act(mag, drew, AF.Exp)
            act(sn, frw, AF.Sin, scale=TWO_PI)
            ts("dve", t1, frw, 0.25, ALU.add)
            wrap("dve", t1, t0)
            act(cs, t1, AF.Sin, scale=TWO_PI)
            tt("dve", DFW[:, 0], mag, cs, ALU.mult)
            tt("dve", DFW[:, 1], mag, sn, ALU.mult)
            ts("dve", cs, DFW[:, 0], -1.0, ALU.add)
            sn = DFW[:, 1]
            tt("dve", mag, aRe, aRe, ALU.mult)
            tt("dve", t0, aIm, aIm, ALU.mult)
            tt("dve", mag, mag, t0, ALU.add)
            P.add("dve", lambda e: e.reciprocal(out=mag.ap, in_=mag.ap), [mag], [mag])
            fre_w, fim_w = G[0][:], G[2][:]
            tt("dve", t0, cs, aRe, ALU.mult)
            tt("dve", t1, sn, aIm, ALU.mult)
            tt("dve", t0, t0, t1, ALU.add)
            tt("dve", fre_w, t0, mag, ALU.mult)
            tt("dve", t0, sn, aRe, ALU.mult)
            tt("dve", t1, cs, aIm, ALU.mult)
            tt("dve", t0, t0, t1, ALU.subtract)
            tt("dve", fim_w, t0, mag, ALU.mult)
            tt("dve", t0, fre_w, bRe, ALU.mult)
            tt("dve", t1, fim_w, bIm, ALU.mult)
            tt("dve", FBW[:, 0], t0, t1, ALU.subtract)
            tt("dve", t0, fre_w, bIm, ALU.mult)
            tt("dve", t1, fim_w, bRe, ALU.mult)
            tt("dve", FBW[:, 1], t0, t1, ALU.add)
            sA = lambda i: SSTATE[:, i]
            S_dt, S_dre, S_fr, S_R8, S_fr8, S_fre, S_fim = (sA(i) for i in range(7))
            u = [sA(i) for i in range(7, 13)]
            aRe_s, aIm_s, lgd_s = COLS[:, c_ss:c_ss + 16], COLS[:, c_ss + 16:c_ss + 32], COLS[:, c_ss + 32:c_ss + 48]
            act(S_dt, lgd_s, AF.Exp)
            tt("dve", S_dre, S_dt, aRe_s, ALU.mult)
            stt(S_fr, aIm_s, 1.0 / TWO_PI, S_dt, ALU.mult, ALU.mult)
            wrap("dve", S_fr, u[0])
            act(u[1], S_dre, AF.Exp)
            act(u[2], S_fr, AF.Sin, scale=TWO_PI)
            ts("dve", u[3], S_fr, 0.25, ALU.add)
            wrap("dve", u[3], u[0])
            act(u[3], u[3], AF.Sin, scale=TWO_PI)
            tt("dve", u[3], u[1], u[3], ALU.mult)
            ts("dve", u[3], u[3], -1.0, ALU.add)
            tt("dve", u[2], u[1], u[2], ALU.mult)
            tt("dve", u[1], aRe_s, aRe_s, ALU.mult)
            tt("dve", u[0], aIm_s, aIm_s, ALU.mult)
            tt("dve", u[1], u[1], u[0], ALU.add)
            P.add("dve", lambda e: e.reciprocal(out=u[1].ap, in_=u[1].ap), [u[1]], [u[1]])
            tt("dve", u[0], u[3], aRe_s, ALU.mult)
            tt("dve", u[4], u[2], aIm_s, ALU.mult)
            tt("dve", u[0], u[0], u[4], ALU.add)
            tt("dve", S_fre, u[0], u[1], ALU.mult)
            tt("dve", u[0], u[2], aRe_s, ALU.mult)
            tt("dve", u[4], u[3], aIm_s, ALU.mult)
            tt("dve", u[0], u[0], u[4], ALU.subtract)
            tt("dve", S_fim, u[0], u[1], ALU.mult)
            ts("dve", u[0], S_dre, 8.0, ALU.mult)
            act(S_R8, u[0], AF.Exp)
            ts("dve", S_fr8, S_fr, 8.0, ALU.mult)
            wrap("dve", S_fr8, u[0])
            B0, B1 = PRMB[:, :, 0], PRMB[:, :, 1]
            fr_b = bc(S_fre, 2, [128, 16, 32])
            fi_b = bc(S_fim, 2, [128, 16, 32])
            X0 = mk(T1o + 2048, [16, 32], F32)
            X1 = mk(T1o, [16, 32], F32)
            tt("dve", X0[:], fr_b, B0, ALU.mult)
            tt("dve", X1[:], fi_b, B1, ALU.mult)
            tt("dve", FBT[:, :, 0], X0[:], X1[:], ALU.subtract)
            tt("dve", X0[:], fr_b, B1, ALU.mult)
            tt("dve", X1[:], fi_b, B0, ALU.mult)
            tt("dve", FBT[:, :, 1], X0[:], X1[:], ALU.add)
            PHk = mk(T1o + 4096, [16, 9], F32)
            PHc = mk(T1o + 4096 + 576, [16, 9], F32)
            RNk = mk(T1o + 4096 + 1152, [16, 9], F32)
            MGk = mk(T1o + 4096 + 1728, [16, 9], F32)
            k9 = bc(IDX[:, 0:9], 1, [128, 16, 9])
            tt("dve", PHk[:], bc(S_fr, 2, [128, 16, 9]), k9, ALU.mult)
            wrap("dve", PHk[:], RNk[:])
            ts("dve", PHc[:], PHk[:], 0.25, ALU.add)
            wrap("dve", PHc[:], RNk[:])
            tt("dve", MGk[:], bc(S_dre, 2, [128, 16, 9]), k9, ALU.mult)
            act(MGk[:], MGk[:], AF.Exp)
            act(PHk[:], PHk[:], AF.Sin, scale=TWO_PI)
            act(PHc[:], PHc[:], AF.Sin, scale=TWO_PI)
            tt("dve", PK[:, 0], MGk[:], PHc[:], ALU.mult)
            tt("dve", PK[:, 1], MGk[:], PHk[:], ALU.mult)

        def capture(fn):
            keep = P.ops
            P.ops = []
            fn()
            got = P.ops
            P.ops = keep
            return got

        def layer(l):
            cb = l * LC
            c_g1, c_g2, c_bg, c_bglu, c_cb, c_lng, c_lnb, c_psc, c_dsk, c_ss = (
                cb, cb + 8, cb + 16, cb + 40, cb + 44, cb + 48, cb + 52, cb + 56, cb + 60, cb + 64)
            win = d_win[l].rearrange("(kt p) n -> p kt n", p=128)

            if l == 0 or not TAIL_NORM:
                for tb in range(NTB):
                    rmsnorm(tb, c_g1, H)
            if l == 0:
                dump("h1", H[:], [NTB, KT, 512], BF16)

            if not HOIST:
                ssm_layer_gen(l)
            memset("dve", W1PAD[:], 0.0)
            memset("dve", KMAT[:], 0.0)
            memset("dve", HPREV[:], 0.0)

            chk(1)
            def gen_w1(pt):
                W1F = mk(T1o, [8, 2, 128], F32)
                P1 = mk(T1o + 8192, [2, 128], F32)
                P2 = mk(T1o + 9216, [2, 128], F32)
                ar_b = bc(DFW[:, 0, pt], 1, [128, 2, 128])
                ai_b = bc(DFW[:, 1, pt], 1, [128, 2, 128])
                P.add("dve", (lambda e, pt=pt: e.tensor_copy(out=W1F[:, 7].ap, in_=FBW[:, :, pt].ap)), [W1F[:, 7]], [FBW[:, :, pt]])
                ar1, ai1 = DFW[:, 0, pt], DFW[:, 1, pt]
                for sg_ in range(6, -1, -1):
                    wre, wim = W1F[:, sg_ + 1, 0], W1F[:, sg_ + 1, 1]
                    tt("dve", P1[:, 0], ar1, wre, ALU.mult)
                    tt("dve", P2[:, 0], ai1, wre, ALU.mult)
                    tt("dve", P2[:, 1], ai1, wim, ALU.mult)
                    tt("dve", P1[:, 1], ar1, wim, ALU.mult)
                    tt("dve", W1F[:, sg_, 0], P1[:, 0], P2[:, 1], ALU.subtract)
                    tt("dve", W1F[:, sg_, 1], P1[:, 1], P2[:, 0], ALU.add)
                for q4 in range(4):
                    dst = W1PAD[32 * q4:32 * q4 + 32, q4]
                    src = ps_(W1F[:], 32 * q4, 32 * q4 + 32)
                    src = src.w(src.ap.rearrange("p a b c -> p (a b c)"))
                    dstf = dst.w(dst.ap.rearrange("p a b c -> p (a b c)"))
                    P.add("pool", (lambda e, dstf=dstf, src=src: e.dma_start(out=dstf.ap, in_=src.ap)), [dstf], [src], dma=True)

            def gen_wck(pt):
                chk(2)
                tA = mk(T1o, [4, 9, 32], F32)[:]
                tB = mk(T1o + 4608, [4, 9, 32], F32)[:]
                pkr = bc(PK[:, 0, 4 * pt:4 * pt + 4, :], 3, [128, 4, 9, 32])
                pki = bc(PK[:, 1, 4 * pt:4 * pt + 4, :], 3, [128, 4, 9, 32])
                cr = bc(PRMC[:, 4 * pt:4 * pt + 4, 0, :], 2, [128, 4, 9, 32])
                ci = bc(PRMC[:, 4 * pt:4 * pt + 4, 1, :], 2, [128, 4, 9, 32])
                tt("dve", tA, pkr, cr, ALU.mult)
                tt("dve", tB, pki, ci, ALU.mult)
                tt("dve", WC[:, :, 0, :, :], tA, tB, ALU.subtract)
                tt("dve", tA, pkr, ci, ALU.mult)
                tt("dve", tB, pki, cr, ALU.mult)
                stt(WC[:, :, 1, :, :], tA, -1.0, tB, ALU.mult, ALU.subtract)

            def gen_kmat(pt):
                kb = nb()
                for q4 in range(4):
                    q = 4 * pt + q4
                    mm(kb[32 * q4:32 * q4 + 32, 0:256],
                       [(FBT[:, q, 0], WC[:, q4, 0, 0:8, :]), (FBT[:, q, 1], WC[:, q4, 1, 0:8, :])], tile_position=(0, 32 * q4))
                for q4 in range(4):
                    src = kb[32 * q4:32 * q4 + 32, 0:256]
                    src = src.w(src.ap.rearrange("p (k j) -> p k j", k=8))
                    dst = KMAT[32 * q4:32 * q4 + 32, :, 32 * q4:32 * q4 + 32]
                    act(dst, src, AF.Copy)

            Wssm = [wload([(win[:, :, c * 256:(c + 1) * 256], [KT, 256])])[0] for c in range(2)]
            gen_w1(0)
            gen_wck(0)
            for c in range(2):
                W = Wssm[c]
                for tb in range(NTB):
                    for mm_ in range(2):
                        m = 2 * c + mm_
                        bk = nb()
                        mm(bk[:], [(W[:, kt, mm_ * 128:(mm_ + 1) * 128], H[:, tb, kt]) for kt in range(KT)])
                        act(UA[:, m, tb * 512:(tb + 1) * 512], bk[:], AF.Copy)
            if l == 0:
                dump("ua", UA[:], [4, 2048], BF16)
            for pt in range(4):
                if pt > 0:
                    gen_wck(pt)
                gen_kmat(pt)
                chk(4)
                uvw = UA[:, pt]
                udsrc = uvw.w(uvw.ap.rearrange("p (c s) -> p s c", s=8))
                P.add("dve", (lambda e, udsrc=udsrc: e.tensor_copy(out=UD[:].ap, in_=udsrc.ap)), [UD[:]], [udsrc])
                sbs = []
                for q4 in range(4):
                    sb_ = nb()
                    for ri in range(2):
                        mm(sb_[:, ri * 256:(ri + 1) * 256], [(W1PAD[:, q4, sg_, ri, :], UD[:, sg_]) for sg_ in range(8)])
                    sbs.append(sb_)
                nm = ["rn", "t1", "t2", "bre", "bim", "gre", "gim", "ph0", "phc0"]
                L2 = {n_: mk(T1o + i * 1024, [256], F32)[:] for i, n_ in enumerate(nm)}
                rn, t1_, t2_, bre, bim, gre, gim = (L2[n_] for n_ in nm[:7])
                t3_ = mk(o_MISC + 14848, [256], F32)[:]
                t4_ = rn
                tabs = [(L2["ph0"], L2["phc0"]),
                        (mk(hb + 27904 + 2048, [256], F32)[:], mk(hb + 27904 + 3072, [256], F32)[:])]

                def gen_tables(q4):
                    q = 4 * pt + q4
                    ph, phc = tabs[q4 % 2]
                    ts("dve", ph, IDX[:, 1:257], SSTATE[:, 4, q:q + 1], ALU.mult)
                    wrap("dve", ph, rn)
                    ts("dve", rn, ph, 0.25, ALU.is_gt)
                    stt(phc, ph, 0.25, rn, ALU.add, ALU.subtract)
                    act(ph, ph, AF.Sin, scale=TWO_PI)
                    act(phc, phc, AF.Sin, scale=TWO_PI)

                gen_tables(0)
                for q4 in range(4):
                    q = 4 * pt + q4
                    if q4 + 1 < 4:
                        gen_tables(q4 + 1)
                    ph, phc = tabs[q4 % 2]
                    sb_ = sbs[q4]
                    sre, sim = sb_[:, 0:256], sb_[:, 256:512]
                    tt("dve", t1_, phc, sre, ALU.mult)
                    tt("dve", t2_, ph, sim, ALU.mult)
                    tt("dve", t3_, phc, sim, ALU.mult)
                    tt("dve", t4_, ph, sre, ALU.mult)
                    tt("dve", bre, t1_, t2_, ALU.add)
                    tt("dve", bim, t3_, t4_, ALU.subtract)
                    Rb = SSTATE[:, 3, q:q + 1]
                    for (gg, bb_) in ((gre, bre), (gim, bim)):
                        P.add("dve", (lambda e, gg=gg, bb_=bb_, Rb=Rb: e.tensor_tensor_scan(
                            out=gg.ap, data0=Rb.ap.to_broadcast([128, 256]), data1=bb_.ap, initial=0.0,
                            op0=ALU.mult, op1=ALU.add)), [gg], [bb_, Rb])
                    tt("dve", t1_, phc, gre, ALU.mult)
                    tt("dve", t2_, ph, gim, ALU.mult)
                    tt("dve", t3_, phc, gim, ALU.mult)
                    tt("dve", t4_, ph, gre, ALU.mult)
                    tt("dve", HPREV[:, q4, 0, 0:256], t1_, t2_, ALU.subtract)
                    tt("dve", HPREV[:, q4, 1, 0:256], t3_, t4_, ALU.add)
                if pt + 1 < 4:
                    gen_w1(pt + 1)
                chk(5)
                ybs = [nb() for _ in range(4)]
                for tau in range(8):
                    yo = ybs[tau // 2][:, (tau % 2) * 256:(tau % 2 + 1) * 256]
                    rd = [KMAT[:, 0:tau + 1], UD[:], WC[:], HPREV[:]]

                    def fn(pe, tau=tau, yo=yo, pt=pt):
                        ins = None
                        for sg_ in range(tau + 1):
                            ins = pe.matmul(yo.ap, lhsT=(W1PAD[:, 0, tau - sg_, 0, :].ap if KTEST else KMAT[:, tau - sg_, :].ap), rhs=UD[:, sg_].ap,
                                            start=(sg_ == 0), stop=(NOCARRY and sg_ == tau))
                        for q4 in (range(0) if NOCARRY else range(4)):
                            for ri in range(2):
                                ins = pe.matmul(yo.ap[32 * q4:32 * q4 + 32, 1:256], lhsT=WC[:, q4, ri, tau + 1, :].ap,
                                                rhs=HPREV[:, q4, ri, 0:255].ap, start=False,
                                                stop=(ri == 1), tile_position=(0, 32 * q4))
                        return ins
                    if MMTEST:
                        mm(yo, [(KMAT[:, tau - sg_, :], UD[:, sg_]) for sg_ in range(tau + 1)])
                        for q4 in range(4):
                            yq = yo.w(yo.ap[32 * q4:32 * q4 + 32, 1:256])
                            mm(yq, [(WC[:, q4, ri, tau + 1, :], HPREV[:, q4, ri, 0:255]) for ri in range(2)],
                               tile_position=(0, 32 * q4), start=False)
                    else:
                        P.add("pe", fn, [yo], rd)
                    chk(61 + tau)
                chk(6)
                for b in range(4):
                    uv = UA[:, pt]
                    uv = uv.w(uv.ap.rearrange("p (c s) -> p s c", s=8)[:, 2 * b:2 * b + 2, :])
                    bv = ybs[b][:]
                    bv = bv.w(bv.ap.rearrange("p (s c) -> p s c", s=2))
                    ys = YS[b % 2][:]
                    stt(ys, UD[:, 2 * b:2 * b + 2], col(c_dsk + pt), bv, ALU.mult, ALU.add)
                    act(uv, ys, AF.Gelu_apprx_tanh)
                chk(7)

            if l == 0:
                dump("g", UA[:], [4, 2048], BF16)
            (WGL,) = wload([(d_wglu[l].rearrange("(kt p) n -> p kt n", p=128), [4, 512])])
            SG = mk(o_MISC, [2, 512], F32)
            for tb in range(NTB):
                bks = []
                for m in range(4):
                    bk = nb()
                    mm(bk[:], [(WGL[:, kt, m * 128:(m + 1) * 128], UA[:, kt, tb * 512:(tb + 1) * 512]) for kt in range(4)])
                    bks.append(bk)
                for m in range(4):
                    sg = SG[:, m % 2]
                    act(sg, bks[m][:], AF.Sigmoid, bias=col(c_bglu + m), scale=1.0)
                    tt("dve", UA[:, m, tb * 512:(tb + 1) * 512], UA[:, m, tb * 512:(tb + 1) * 512], sg, ALU.mult)

            if l == 0:
                dump("s5", UA[:], [4, 2048], BF16)
            for k in range(4):
                memset("dve", UC[:, k, 0:16], 0.0)
                memset("dve", HC[:, k, 0:30], 0.0)
            for c in range(2):
                (W,) = wload([(win[:, :, 1536 + c * 256:1536 + (c + 1) * 256], [KT, 256])])
                for tb in range(NTB):
                    for mm_ in range(2):
                        m = 2 * c + mm_
                        bk = nb()
                        mm(bk[:], [(W[:, kt, mm_ * 128:(mm_ + 1) * 128], H[:, tb, kt]) for kt in range(KT)])
                        act(UC[:, m, 16 + tb * 512:16 + tb * 512 + 512], bk[:], AF.Copy)
            (WGR,) = wload([(d_wgrp[l].rearrange("g c d -> c g d"), [4, 128])])
            PW = mk(o_MISC, [2, 528], F32)
            PP = mk(o_MISC + 4224, [2, 512], BF16)
            TW = mk(o_MISC + 6272, [16], F32)
            pst = {"it": 0}

            def pool_dve(k, tb):
                w = 2 ** (k + 1)
                b0 = tb * 512
                U = lambda a, b_: UC[:, k, b0 + a:b0 + b_]
                tt("dve", PW[:, 0, 1:528], U(1, 528), U(0, 527), ALU.add)
                fin = 0
                if k >= 1:
                    tt("dve", PW[:, 1, 3:528], PW[:, 0, 3:528], PW[:, 0, 1:526], ALU.add)
                    fin = 1
                if k >= 2:
                    tt("dve", PW[:, 0, 7:528], PW[:, 1, 7:528], PW[:, 1, 3:524], ALU.add)
                    fin = 0
                if k >= 3:
                    tt("dve", PW[:, 1, 15:528], PW[:, 0, 15:528], PW[:, 0, 7:520], ALU.add)
                    fin = 1
                pp = PP[:, pst["it"] % 2]
                pst["it"] += 1
                stt(pp, PW[:, fin, 16:528], 1.0 / w, U(16, 528), ALU.mult, ALU.subtract)
                if tb == 0:
                    tt("dve", TW[:], PW[:, fin, 16:32], INVT[:, k], ALU.mult)
                    tt("dve", pp.w(pp.ap[:, 0:16]), TW[:], U(16, 32), ALU.subtract)
                return pp

            def pool_pe(k, tb, pp):
                b0 = tb * 512
                bk = nb()
                mm(bk[:], [(WGR[:, k], pp)])
                ts("dve", UC[:, k, b0 + 16:b0 + 528], bk[:], col(c_psc + k), ALU.mult)

            items = [(k, tb) for k in range(4) for tb in (3, 2, 1, 0)]
            SGc = mk(o_MISC + 8192, [2, 512], F32)
            gi = 0
            for c in range(2):
                (WA,) = wload([(win[:, :, 512 + c * 256:512 + (c + 1) * 256], [KT, 256])])
                (WB,) = wload([(win[:, :, 1024 + c * 256:1024 + (c + 1) * 256], [KT, 256])])
                for tb in range(NTB):
                    for mm_ in range(2):
                        m = 2 * c + mm_
                        pk_, ptb_ = items[gi]
                        pp = pool_dve(pk_, ptb_)
                        ba, bb_ = nb(), nb()
                        mm(ba[:], [(WA[:, kt, mm_ * 128:(mm_ + 1) * 128], H[:, tb, kt]) for kt in range(KT)])
                        mm(bb_[:], [(WB[:, kt, mm_ * 128:(mm_ + 1) * 128], H[:, tb, kt]) for kt in range(KT)])
                        pool_pe(pk_, ptb_, pp)
                        sg = SGc[:, gi % 2]
                        act(sg, bb_[:], AF.Sigmoid)
                        tt("dve", HC[:, m, 30 + tb * 512:30 + tb * 512 + 512], ba[:], sg, ALU.mult)
                        gi += 1

            if l == 0:
                dump("poolout", UC[:], [4, 2064], BF16)
            if l == 0:
                dump("hc", HC[:], [4, 2078], BF16)
            CSQ = mk(o_MISC, [4, 512], BF16)
            MU = mk(o_MISC + 4096, [2, 512], F32)
            RSTD = mk(o_MISC + 8192, [2, 512], F32)
            M2 = mk(o_MISC + 12288, [512], F32)
            T1 = mk(o_MISC + 14336, [512], F32)
            for hf in (1, 0):
                for m in range(4):
                    (DA,) = wload([(d_convd[l, m, :, 0:16, :], [16, 128])])
                    (DB,) = wload([(d_convd[l, m, :, 16:31, :], [15, 128])])
                    for tbl in range(2):
                        tb = 2 * hf + tbl
                        bk = nb()
                        prs = [((DA[:, k] if k < 16 else DB[:, k - 16]), HC[:, m, tb * 512 + k:tb * 512 + k + 512]) for k in range(31)]
                        mm(bk[:], prs)
                        act(CV[:, tbl, m], bk[:], AF.Identity, bias=col(c_cb + m), scale=1.0)
                for tbl in range(2):
                    act(CSQ[:], CV[:, tbl], AF.Square)
                    bmu, bsq = nb(), nb()
                    mm(bmu[:], [(ONESF[:], CV[:, tbl, m]) for m in range(4)])
                    mm(bsq[:], [(ONESB[:], CSQ[:, m]) for m in range(4)])
                    act(MU[:, tbl], bmu[:], AF.Identity)
                    tt("dve", M2[:], MU[:, tbl], MU[:, tbl], ALU.mult)
                    stt(M2[:], bsq[:], 2.0, M2[:], ALU.mult, ALU.subtract)
                    act(RSTD[:, tbl], M2[:], AF.Sqrt, bias=EPSC[:], scale=1.0)
                    P.add("dve", (lambda e, tbl=tbl: e.reciprocal(out=RSTD[:, tbl].ap, in_=RSTD[:, tbl].ap)), [RSTD[:, tbl]], [RSTD[:, tbl]])
                for tbl in range(2):
                    tb = 2 * hf + tbl
                    for m in range(4):
                        tq = (T1, M2)[m % 2]
                        tt("dve", tq[:], CV[:, tbl, m], MU[:, tbl], ALU.subtract)
                        tt("dve", tq[:], tq[:], RSTD[:, tbl], ALU.mult)
                        act(HC[:, m, 30 + tb * 512:30 + tb * 512 + 512], tq[:], AF.Silu, bias=col(c_lnb + m), scale=col(c_lng + m))

            if l == 0:
                dump("convout", HC[:], [4, 2078], BF16)
            SGm = mk(o_MISC, [2, 512], F32)
            ACC = mk(o_MISC + 4096, [2, 512], F32)
            TMP = mk(o_MISC + 8192, [2, 512], F32)
            projs = (d_wpa, d_wpb, d_wpc)

            def branch(k, tb, kt):
                if k == 0:
                    return UA[:, kt, tb * 512:(tb + 1) * 512]
                if k == 1:
                    return HC[:, kt, 30 + tb * 512:30 + tb * 512 + 512]
                return UC[:, kt, 16 + tb * 512:16 + tb * 512 + 512]

            it = 0
            for hf in (1, 0):
                for j in range(8):
                    ws = []
                    for k in range(3):
                        gsrc = win[:, :, 2048 + k * 1024 + j * 128:2048 + k * 1024 + (j + 1) * 128]
                        psrc = projs[k][l].rearrange("(kt p) n -> p kt n", p=128)[:, :, j * 128:(j + 1) * 128]
                        ws.append(wload([(gsrc, [KT, 128]), (psrc, [4, 128])]))
                    for tbl in range(2):
                        tb = 2 * hf + tbl
                        par = it % 2
                        it += 1
                        for k in range(3):
                            WG_, WP_ = ws[k]
                            bg, by = nb(), nb()
                            mm(bg[:], [(WG_[:, kt], H[:, tb, kt]) for kt in range(KT)])
                            mm(by[:], [(WP_[:, kt], branch(k, tb, kt)) for kt in range(4)])
                            sg = SGm[:, k % 2]
                            act(sg, bg[:], AF.Sigmoid, bias=col(c_bg + k * 8 + j), scale=1.0)
                            if k == 0:
                                tt("dve", ACC[:, par], by[:], sg, ALU.mult)
                            else:
                                tt("dve", TMP[:, k - 1], by[:], sg, ALU.mult)
                        tt("dve", ACC[:, par], ACC[:, par], TMP[:, 0], ALU.add)
                        tt("dve", MERGED[:, tbl, j], ACC[:, par], TMP[:, 1], ALU.add)
                wo = d_wout[l].rearrange("(kt p) n -> p kt n", p=128)
                for dc in range(4):
                    (WO,) = wload([(wo[:, :, dc * 256:(dc + 1) * 256], [KT, 256])])
                    for tbl in range(2):
                        tb = 2 * hf + tbl
                        for dd in range(2):
                            dp = 2 * dc + dd
                            bk = nb()
                            mm(bk[:], [(WO[:, j, dd * 128:(dd + 1) * 128], MERGED[:, tbl, j]) for j in range(8)])
                            tt("dve", X[:, tb, dp], bk[:], X[:, tb, dp], ALU.add)

            if l == 0:
                dump("xmix", X[:], [NTB, KT, 512], F32)
            for tb in range(NTB):
                rmsnorm(tb, c_g2, H)
            wg = d_wg[l].rearrange("(kt p) n -> p kt n", p=128)
            wu = d_wu[l].rearrange("(kt p) n -> p kt n", p=128)
            SGf = mk(o_MISC, [2, 512], F32)
            it = 0
            pend = capture(lambda: ssm_layer_gen(l + 1)) if (HOIST and l + 1 < NL) else []
            for (f0, nf) in ((0, 8), (8, 8), (16, 6)):
                for fi in range(nf):
                    f = f0 + fi
                    WG_, WU_ = wload([(wg[:, :, f * 128:(f + 1) * 128], [KT, 128]), (wu[:, :, f * 128:(f + 1) * 128], [KT, 128])])
                    for tb in range(NTB):
                        bg, bu = nb(), nb()
                        mm(bg[:], [(WG_[:, kt], H[:, tb, kt]) for kt in range(KT)])
                        mm(bu[:], [(WU_[:, kt], H[:, tb, kt]) for kt in range(KT)])
                        sg = SGf[:, it % 2]
                        it += 1
                        act(sg, bg[:], AF.Silu)
                        tt("dve", ACTB[:, tb, fi], bu[:], sg, ALU.mult)
                        if f0 >= 8:
                            for _ in range(3):
                                if pend:
                                    P.ops.append(pend.pop(0))
                wds = []
                for s in range(nf // 2):
                    (WD,) = wload([(d_wd[l, (f0 + 2 * s) * 128:(f0 + 2 * s + 2) * 128, :].rearrange("(ft p) n -> p ft n", p=128), [2, 1024])])
                    wds.append(WD)
                def down(dp, tb, nf=nf, wds=wds):
                    bk = nb()
                    mm(bk[:], [(wds[fi // 2][:, fi % 2, dp * 128:(dp + 1) * 128], ACTB[:, tb, fi]) for fi in range(nf)])
                    tt("dve", X[:, tb, dp], bk[:], X[:, tb, dp], ALU.add)

                if TAIL_NORM and f0 == 16 and not STOP:
                    def next_norm(tb):
                        if l == 0 and DEBUG:
                            dump("xl0_%d" % tb, X[:, tb], [KT, 512], F32)
                        if l + 1 < NL:
                            rmsnorm(tb, (l + 1) * LC, H)
                        else:
                            rmsnorm(tb, NL * LC, X)
                            P.add("sp", (lambda e, tb=tb: e.dma_start(out=d_out[:, tb], in_=X[:, tb].ap)), [], [X[:, tb]], dma=True, final=True)
                    for tb in range(NTB):
                        for dp in range(8):
                            down(dp, tb)
                        if tb >= 1:
                            next_norm(tb - 1)
                    next_norm(NTB - 1)
                else:
                    for dp in range(8):
                        for tb in range(NTB):
                            down(dp, tb)

        try:
            if HOIST:
                ssm_layer_gen(0)
            for l in range(NL):
                layer(l)
                if l == 0 and not TAIL_NORM:
                    dump("xl0", X[:], [NTB, KT, 512], F32)
        except _Stop:
            pass
        if not TAIL_NORM or STOP:
            for tb in range(NTB):
                rmsnorm(tb, NL * LC, X)
                P.add("sp", (lambda e, tb=tb: e.dma_start(out=d_out[:, tb], in_=X[:, tb].ap)), [], [X[:, tb]], dma=True, final=True)

    ctx["body"] = body
    ctx["P"] = P
    ctx["TOTAL"] = TOTAL
    return nc, ctx


def emit(nc, ctx):
    P = ctx["P"]
    ENG = ["pe", "act", "dve", "pool", "sp"]
    with (
        nc.sbuf_tensor("raw", [128, ctx["TOTAL"] // 4], F32) as RAW,
        nc.psum_tensor("ps", [128, 4096], F32) as PS,
    ):
        ctx["body"](RAW, PS)
        ops = P.ops
        n = len(ops)
        trk = {"S": Track(), "P": Track()}
        deps = []
        for i, (eng, fn, outs, ins, dma, final) in enumerate(ops):
            d = set()
            for v in ins:
                d |= trk[v.sp].access(v.lo, v.hi, i, eng, dma, False)
            for v in outs:
                d |= trk[v.sp].access(v.lo, v.hi, i, eng, dma, True)
            d.discard(i)
            deps.append(d)
        signal = [False] * n
        for i in range(n):
            ei, di = ops[i][0], ops[i][4]
            for j in deps[i]:
                ej, dj = ops[j][0], ops[j][4]
                if dj:
                    continue
                if ej == ei and not di and (ei == "pe" or not SAME_SYNC):
                    continue
                signal[j] = True
        cnt = {e: 0 for e in ENG}
        sigval = [0] * n
        dq = {"sp": 0, "pool": 0}
        uses = {(q, s): 0 for q in dq for s in range(NDMASEM)}
        dslot = [None] * n
        dval = [0] * n
        for i in range(n):
            eng, _, _, _, dma, _ = ops[i]
            if dma:
                s = dq[eng] % NDMASEM
                dq[eng] += 1
                uses[(eng, s)] += 1
                dslot[i] = (eng, s)
                dval[i] = 16 * uses[(eng, s)]
            elif signal[i]:
                cnt[eng] += 1
                sigval[i] = cnt[eng]
        import contextlib
        with contextlib.ExitStack() as es:
            esem = {e: es.enter_context(nc.semaphore("s_" + e)) for e in ENG}
            dsem = {(q, s): es.enter_context(nc.semaphore(f"d_{q}{s}")) for q in dq for s in range(NDMASEM)}
            block = es.enter_context(nc.Block())

            def stream(ename, handle):
                waited = {}

                def wait(sem, key, val):
                    if val <= 0 or waited.get(key, 0) >= val:
                        return
                    waited[key] = val
                    handle.wait_ge(sem, val)

                finals = []
                for i in range(n):
                    eng, fn, outs, ins, dma, final = ops[i]
                    if eng != ename:
                        continue
                    for j in sorted(deps[i]):
                        ej, dj = ops[j][0], ops[j][4]
                        if dj:
                            wait(dsem[dslot[j]], ("d",) + dslot[j], dval[j])
                        elif ej == ename and not dma and (ename == "pe" or not SAME_SYNC):
                            continue
                        else:
                            wait(esem[ej], ("e", ej), sigval[j])
                    if dma:
                        wait(dsem[dslot[i]], ("d",) + dslot[i], dval[i] - 16)
                    ins_ = fn(handle)
                    if dma:
                        ins_.then_inc(dsem[dslot[i]], 16)
                        if final:
                            finals.append(i)
                    elif signal[i]:
                        ins_.then_inc(esem[ename], 1)
                for i in finals:
                    wait(dsem[dslot[i]], ("d",) + dslot[i], dval[i])

            @block.sync
            def _(h):
                stream("sp", h)

            @block.gpsimd
            def _(h):
                stream("pool", h)

            @block.scalar
            def _(h):
                stream("act", h)

            @block.vector
            def _(h):
                stream("dve", h)

            @block.tensor
            def _(h):
                stream("pe", h)
    return nc


def _prep_shared(inp):
    f = lambda a: np.ascontiguousarray(np.asarray(a, dtype=np.float32))
    cols = np.zeros((128, NCOL), np.float32)

    def tiles(v, nt):
        return np.asarray(v, np.float32).reshape(nt, 128).T

    for l in range(NL):
        cb = l * LC
        cols[:, cb:cb + 8] = tiles(inp["norm1"][l], 8)
        cols[:, cb + 8:cb + 16] = tiles(inp["norm2"][l], 8)
        cols[:, cb + 16:cb + 40] = tiles(inp["b_gate"][l], 24)
        cols[:, cb + 40:cb + 44] = tiles(inp["ssm_b_glu"][l], 4)
        cols[:, cb + 44:cb + 48] = tiles(inp["conv_b_dw"][l], 4)
        cols[:, cb + 48:cb + 52] = tiles(inp["conv_ln_g"][l], 4)
        cols[:, cb + 52:cb + 56] = tiles(inp["conv_ln_b"][l], 4)
        cols[:, cb + 56:cb + 60] = tiles(inp["pool_scale"][l], 4)
        cols[:, cb + 60:cb + 64] = tiles(np.asarray(inp["ssm_d"][l]).reshape(512), 4)
        are = np.asarray(inp["ssm_a_re"][l], np.float32)
        aim = np.asarray(inp["ssm_a_im"][l], np.float32)
        ldt = np.asarray(inp["ssm_log_dt"][l], np.float32)
        cols[:, cb + 64:cb + 80] = are.reshape(16, 2, 64).transpose(1, 2, 0).reshape(128, 16)
        cols[:, cb + 80:cb + 96] = aim.reshape(16, 2, 64).transpose(1, 2, 0).reshape(128, 16)
        cols[:, cb + 96:cb + 112] = np.broadcast_to(ldt.reshape(16, 2, 1), (16, 2, 64)).transpose(1, 2, 0).reshape(128, 16)
    cols[:, NL * LC:NL * LC + 8] = tiles(inp["final_norm"], 8)

    ssmw = np.zeros((NL, 128, 5, 4, 128), np.float32)
    ssmc = np.zeros((NL, 128, 16, 2, 32), np.float32)
    ssmb = np.zeros((NL, 128, 16, 2, 32), np.float32)
    for l in range(NL):
        are = np.asarray(inp["ssm_a_re"][l], np.float32)
        aim = np.asarray(inp["ssm_a_im"][l], np.float32)
        ldt = np.asarray(inp["ssm_log_dt"][l], np.float32)
        bre = np.asarray(inp["ssm_b_re"][l], np.float32)
        bim = np.asarray(inp["ssm_b_im"][l], np.float32)
        cre = np.asarray(inp["ssm_c_re"][l], np.float32)
        cim = np.asarray(inp["ssm_c_im"][l], np.float32)
        for pt in range(4):
            for q4 in range(4):
                q = 4 * pt + q4
                rows = slice(32 * q4, 32 * q4 + 32)
                for g2p in range(2):
                    g = 2 * q + g2p
                    cs = slice(64 * g2p, 64 * g2p + 64)
                    ssmw[l, rows, 0, pt, cs] = are[g][None, :]
                    ssmw[l, rows, 1, pt, cs] = aim[g][None, :]
                    ssmw[l, rows, 2, pt, cs] = ldt[g]
                    r2 = slice(32 * q4 + 16 * g2p, 32 * q4 + 16 * g2p + 16)
                    ssmw[l, r2, 3, pt, cs] = bre[g].T
                    ssmw[l, r2, 4, pt, cs] = bim[g].T
        for q in range(16):
            for g2 in range(2):
                g = 2 * q + g2
                ssmc[l, 64 * g2:64 * g2 + 64, q, 0, 16 * g2:16 * g2 + 16] = cre[g].T
                ssmc[l, 64 * g2:64 * g2 + 64, q, 1, 16 * g2:16 * g2 + 16] = cim[g].T
                ssmb[l, 64 * g2:64 * g2 + 64, q, 0, 16 * g2:16 * g2 + 16] = bre[g]
                ssmb[l, 64 * g2:64 * g2 + 64, q, 1, 16 * g2:16 * g2 + 16] = bim[g]
    cw = np.asarray(inp["conv_w_dw"], np.float32)[:, :, 0, :]
    convd = np.zeros((NL, 4, 128, 31, 128), np.float32)
    ar = np.arange(128)
    for l in range(NL):
        for m in range(4):
            convd[l, m, ar, :, ar] = cw[l, :, m * 128:(m + 1) * 128].T
    sh = {
        "cols": cols, "ssmw": ssmw, "ssmc": ssmc, "ssmb": ssmb, "convd": convd,
        "w_in": f(inp["w_in"]), "w_glu": f(inp["ssm_w_glu"]), "w_pa": f(inp["ssm_w_proj"]),
        "w_pb": f(inp["conv_w_proj"]), "w_pc": f(inp["pool_w_proj"]), "w_grp": f(inp["pool_w_group"]),
        "w_out": f(inp["w_out"]), "w_g": f(inp["ffn_w_gate"]), "w_u": f(inp["ffn_w_up"]), "w_d": f(inp["ffn_w_down"]),
    }
    return sh


def kernel(**inputs):
    x = np.asarray(inputs["x"], np.float32)
    sh = _prep_shared(inputs)
    nc, ctx = build_program()
    emit(nc, ctx)
    in_maps = []
    for c in range(8):
        xt = x[c].reshape(NTB, 512, KT, 128).transpose(3, 0, 2, 1)
        m = dict(sh)
        m["xT"] = np.ascontiguousarray(xt)
        in_maps.append(m)
    res = run_bass_kernel_spmd(nc, in_maps, core_ids=list(range(8)))
    out = np.empty((8, T, 1024), np.float32)
    for c in range(8):
        o = np.asarray(res.results[c]["outT"], np.float32)
        out[c] = o.transpose(1, 3, 2, 0).reshape(T, 1024)
    return out
```
